# Optimizing a Trainium2 kernel written in Bass

```python
import jax, jax.numpy as jnp
from jax import lax
import numpy as np

D_MODEL = 2048
BATCH = 4
SEQ = 8192
DEPTH = 1
DEC_BATCH = 32
DEC_SEQ = 16
PAST_LEN = 2048

CHUNK = 64
SGU_CHUNK = 128
SGU_GROUPS = 8
D_SGU = D_MODEL
SGU_GROUP_DIM = D_SGU // SGU_GROUPS
SSM_EXPAND = 2
D_SSM = SSM_EXPAND * D_MODEL
SSM_HEADDIM = 64
SSM_HEADS = D_SSM // SSM_HEADDIM
SSM_GROUPS = 8
SSM_HPG = SSM_HEADS // SSM_GROUPS
SSM_STATE = 128
CONV_WIDTH = 4
CONV_DIM = D_SSM + 2 * SSM_GROUPS * SSM_STATE
SSM_BLOCK = CHUNK
N_BRANCH = 2
D_IN = 2 * D_SGU + D_SSM + CONV_DIM + SSM_HEADS + N_BRANCH * D_MODEL
D_FF = ((8 * D_MODEL // 3 + 255) // 256) * 256
EPS = 1e-6

kernel_name = "hybrid_sgu_ssd_streaming_step"


def _rmsnorm(x, g):
    xf = x.astype(jnp.float32)
    y = xf * lax.rsqrt(jnp.mean(xf * xf, axis=-1, keepdims=True) + EPS)
    return (y * g.astype(jnp.float32)).astype(x.dtype)


def _layernorm(x, g, b):
    xf = x.astype(jnp.float32)
    mu = jnp.mean(xf, axis=-1, keepdims=True)
    var = jnp.mean(jnp.square(xf - mu), axis=-1, keepdims=True)
    y = (xf - mu) * lax.rsqrt(var + EPS)
    return (y * g.astype(jnp.float32) + b.astype(jnp.float32)).astype(x.dtype)


def _sgu_mix(v, w_s, b_s):
    bsz, seq = v.shape[:2]
    q = min(seq, SGU_CHUNK)
    mask = jnp.tril(jnp.ones((q, q), dtype=w_s.dtype))
    w = w_s[:, :q, :q] * mask
    vc = v.reshape(bsz, seq // q, q, SGU_GROUPS, SGU_GROUP_DIM)
    out = jnp.einsum('gts,bcsgd->bctgd', w, vc) + b_s[:, :q].T[None, None, :, :, None]
    return out.reshape(bsz, seq, D_SGU)


def _causal_conv(xbc, buf, w, b):
    seq = xbc.shape[1]
    xpad = jnp.concatenate([buf.astype(xbc.dtype), xbc], axis=1)
    out = b
    for k in range(CONV_WIDTH):
        out = out + xpad[:, k:k + seq] * w[k]
    return out, xpad[:, seq:]


def _ssd(x, dt, a, bmat, cmat, h0, block):
    f32 = jnp.float32
    bsz, seq = x.shape[:2]
    nc = seq // block
    xdt = (x.astype(f32) * dt[..., None]).reshape(bsz, nc, block, SSM_GROUPS, SSM_HPG, SSM_HEADDIM)
    da = (dt * a).reshape(bsz, nc, block, SSM_GROUPS, SSM_HPG)
    bm = bmat.astype(f32).reshape(bsz, nc, block, SSM_GROUPS, SSM_STATE)
    cm = cmat.astype(f32).reshape(bsz, nc, block, SSM_GROUPS, SSM_STATE)
    xs_in = tuple(jnp.moveaxis(t, 1, 0) for t in (xdt, da, bm, cm))
    mask = jnp.tril(jnp.ones((block, block), dtype=bool))[None, :, :, None, None]

    def step(h, inp):
        xc, dac, bc, cc = inp
        cum = jnp.cumsum(dac, axis=1)
        seg = cum[:, :, None] - cum[:, None, :]
        decay = jnp.exp(jnp.where(mask, seg, -jnp.inf))
        cb = jnp.einsum('btgn,bsgn->btsg', cc, bc)
        y_in = jnp.einsum('btsg,btsgh,bsghp->btghp', cb, decay, xc)
        y_st = jnp.einsum('btgn,bghpn->btghp', cc, h) * jnp.exp(cum)[..., None]
        tail = jnp.exp(cum[:, -1:] - cum)
        h_new = h * jnp.exp(cum[:, -1])[..., None, None] + jnp.einsum('bsgn,bsgh,bsghp->bghpn', bc, tail, xc)
        return h_new, y_in + y_st

    h0r = h0.astype(f32).reshape(bsz, SSM_GROUPS, SSM_HPG, SSM_HEADDIM, SSM_STATE)
    h_t, ys = lax.scan(step, h0r, xs_in)
    ys = jnp.moveaxis(ys, 0, 1).reshape(bsz, seq, SSM_HEADS, SSM_HEADDIM)
    return ys, h_t.reshape(bsz, SSM_HEADS, SSM_HEADDIM, SSM_STATE)


def _mixer(h, ssm_h0, conv_buf, w_in, ln_g, ln_b, sgu_w, sgu_b, w_a, conv_w, conv_b,
           dt_bias, a_log, d_skip, ssm_norm_g, w_b, w_o):
    bsz, seq = h.shape[:2]
    proj = h @ w_in
    s0 = 2 * D_SGU
    s1 = s0 + D_SSM
    s2 = s1 + CONV_DIM
    s3 = s2 + SSM_HEADS
    uv, z, xbc, dt_raw, gates = jnp.split(proj, [s0, s1, s2, s3], axis=-1)
    uv = jax.nn.gelu(uv)
    u, v = jnp.split(uv, 2, axis=-1)
    v = _layernorm(v, ln_g, ln_b)
    y_a = (u * _sgu_mix(v, sgu_w, sgu_b)) @ w_a
    xbc, new_buf = _causal_conv(xbc, conv_buf, conv_w, conv_b)
    xbc = jax.nn.silu(xbc)
    xs, bmat, cmat = jnp.split(xbc, [D_SSM, D_SSM + SSM_GROUPS * SSM_STATE], axis=-1)
    xs = xs.reshape(bsz, seq, SSM_HEADS, SSM_HEADDIM)
    bmat = bmat.reshape(bsz, seq, SSM_GROUPS, SSM_STATE)
    cmat = cmat.reshape(bsz, seq, SSM_GROUPS, SSM_STATE)
    dt = jax.nn.softplus(dt_raw.astype(jnp.float32) + dt_bias.astype(jnp.float32))
    a = -jnp.exp(a_log.astype(jnp.float32))
    y_ssd, new_h = _ssd(xs, dt, a, bmat, cmat, ssm_h0, min(SSM_BLOCK, seq))
    y_ssd = y_ssd + d_skip.astype(jnp.float32)[:, None] * xs.astype(jnp.float32)
    y = y_ssd.reshape(bsz, seq, D_SSM).astype(h.dtype) * jax.nn.silu(z)
    y = _rmsnorm(y.reshape(bsz, seq, SSM_GROUPS, D_SSM // SSM_GROUPS),
                 ssm_norm_g.reshape(SSM_GROUPS, D_SSM // SSM_GROUPS)).reshape(bsz, seq, D_SSM)
    y_b = y @ w_b
    g_a, g_b = jnp.split(jax.nn.sigmoid(gates), 2, axis=-1)
    out = (g_a * y_a + g_b * y_b) @ w_o
    return out, new_h.astype(ssm_h0.dtype), new_buf, v


def _layer(x, ssm_h0, conv_buf, norm_mix_g, w_in, ln_g, ln_b, sgu_w, sgu_b, w_a, conv_w, conv_b,
           dt_bias, a_log, d_skip, ssm_norm_g, w_b, w_o, norm_ffn_g, w_gate, w_up, w_down):
    mix, new_h, new_buf, v = _mixer(_rmsnorm(x, norm_mix_g), ssm_h0, conv_buf, w_in, ln_g, ln_b,
                                    sgu_w, sgu_b, w_a, conv_w, conv_b, dt_bias, a_log, d_skip,
                                    ssm_norm_g, w_b, w_o)
    x = x + mix
    hf = _rmsnorm(x, norm_ffn_g)
    x = x + (jax.nn.silu(hf @ w_gate) * (hf @ w_up)) @ w_down
    return x, new_h, new_buf, v


def setup_inputs(seed: int = 0) -> dict:
    key = jax.random.key(seed)
    ks = jax.random.split(key, 24)
    f32 = jnp.float32
    nrm = lambda k, shape, scale: jax.random.normal(k, shape, f32) * scale
    dt0 = jnp.exp(jax.random.uniform(ks[10], (DEPTH, SSM_HEADS), f32, np.log(1e-3), np.log(1e-1)))
    return {
        "x_prompt": nrm(ks[0], (BATCH, SEQ, D_MODEL), 1.0),
        "x_sample": nrm(ks[1], (DEC_BATCH, DEC_SEQ, D_MODEL), 1.0),
        "state_ssm": nrm(ks[2], (DEPTH, DEC_BATCH, SSM_HEADS, SSM_HEADDIM, SSM_STATE), 0.1),
        "state_conv": nrm(ks[3], (DEPTH, DEC_BATCH, CONV_WIDTH - 1, CONV_DIM), 1.0),
        "norm_mix_g": 1.0 + nrm(ks[4], (DEPTH, D_MODEL), 0.02),
        "w_in": nrm(ks[5], (DEPTH, D_MODEL, D_IN), D_MODEL ** -0.5),
        "sgu_ln_g": 1.0 + nrm(ks[6], (DEPTH, D_SGU), 0.02),
        "sgu_ln_b": nrm(ks[7], (DEPTH, D_SGU), 0.02),
        "sgu_w": nrm(ks[8], (DEPTH, SGU_GROUPS, SGU_CHUNK, SGU_CHUNK), SGU_CHUNK ** -0.5),
        "sgu_b": 1.0 + nrm(ks[9], (DEPTH, SGU_GROUPS, SGU_CHUNK), 0.02),
        "w_a": nrm(ks[11], (DEPTH, D_SGU, D_MODEL), D_SGU ** -0.5),
        "conv_w": nrm(ks[12], (DEPTH, CONV_WIDTH, CONV_DIM), CONV_WIDTH ** -0.5),
        "conv_b": nrm(ks[13], (DEPTH, CONV_DIM), 0.02),
        "dt_bias": dt0 + jnp.log(-jnp.expm1(-dt0)),
        "a_log": jnp.log(jax.random.uniform(ks[14], (DEPTH, SSM_HEADS), f32, 1.0, 16.0)),
        "d_skip": 1.0 + nrm(ks[15], (DEPTH, SSM_HEADS), 0.02),
        "ssm_norm_g": 1.0 + nrm(ks[16], (DEPTH, D_SSM), 0.02),
        "w_b": nrm(ks[17], (DEPTH, D_SSM, D_MODEL), D_SSM ** -0.5),
        "w_o": nrm(ks[18], (DEPTH, D_MODEL, D_MODEL), D_MODEL ** -0.5),
        "norm_ffn_g": 1.0 + nrm(ks[19], (DEPTH, D_MODEL), 0.02),
        "w_gate": nrm(ks[20], (DEPTH, D_MODEL, D_FF), D_MODEL ** -0.5),
        "w_up": nrm(ks[21], (DEPTH, D_MODEL, D_FF), D_MODEL ** -0.5),
        "w_down": nrm(ks[22], (DEPTH, D_FF, D_MODEL), D_FF ** -0.5),
        "norm_final_g": 1.0 + nrm(ks[23], (D_MODEL,), 0.02),
    }


def reference(x_prompt, x_sample, state_ssm, state_conv, norm_mix_g, w_in, sgu_ln_g, sgu_ln_b,
              sgu_w, sgu_b, w_a, conv_w, conv_b, dt_bias, a_log, d_skip, ssm_norm_g, w_b, w_o,
              norm_ffn_g, w_gate, w_up, w_down, norm_final_g):
    params = (norm_mix_g, w_in, sgu_ln_g, sgu_ln_b, sgu_w, sgu_b, w_a, conv_w, conv_b, dt_bias,
              a_log, d_skip, ssm_norm_g, w_b, w_o, norm_ffn_g, w_gate, w_up, w_down)
    xp, xs = x_prompt, x_sample
    ssm_p, conv_p, ssm_s, conv_s, v_s = [], [], [], [], []
    for i in range(DEPTH):
        lp = [p[i] for p in params]
        h0_p = jnp.zeros((x_prompt.shape[0], SSM_HEADS, SSM_HEADDIM, SSM_STATE), x_prompt.dtype)
        buf_p = jnp.zeros((x_prompt.shape[0], CONV_WIDTH - 1, CONV_DIM), x_prompt.dtype)
        xp, hp, bp, _ = _layer(xp, h0_p, buf_p, *lp)
        xs, hs, bs, vs = _layer(xs, state_ssm[i], state_conv[i], *lp)
        ssm_p.append(hp)
        conv_p.append(bp)
        ssm_s.append(hs)
        conv_s.append(bs)
        v_s.append(vs)
    y_prompt = _rmsnorm(xp, norm_final_g)
    y_sample = _rmsnorm(xs, norm_final_g)
    new_ssm_prompt = jnp.stack(ssm_p)
    new_conv_prompt = jnp.stack(conv_p)
    new_ssm_sample = jnp.stack(ssm_s)
    new_conv_sample = jnp.stack(conv_s)
    new_sgu_v_sample = jnp.stack(v_s)
    return (y_prompt, y_sample, new_ssm_prompt, new_conv_prompt, new_ssm_sample, new_conv_sample, new_sgu_v_sample)
```

```python
import os
import numpy as np
from contextlib import ExitStack
import concourse.bass as bass
import concourse.mybir as mybir
from concourse.bass_utils import run_bass_kernel_spmd

F32 = mybir.dt.float32
BF16 = mybir.dt.bfloat16
U8 = mybir.dt.uint8
AF = mybir.ActivationFunctionType
ALU = mybir.AluOpType

D = 2048
DS = 4096
CD = 6144
NH = 64
NG = 8
DFF = 5632
DIN = 18496
C_U, C_V, C_Z, C_XS, C_B, C_C, C_DT, C_GA, C_GB = 0, 2048, 4096, 8192, 12288, 13312, 14336, 14400, 16448
EPS = 1e-6
STOP = float(os.environ.get('KSTOP', '99'))
SCHED = int(os.environ.get('KSCHED', '1'))
PE_MHZ = float(os.environ.get('KPEMHZ', '1930'))
WIN = [int(x) for x in os.environ.get('KWIN', '96,32,32,1').split(',')]
LATENCY = float(os.environ.get('KLAT', '0.15'))
NBANKS = tuple(int(x) for x in os.environ.get('KBANKS', '0,1,5,6').split(','))
ENGS = ("pe", "act", "dve", "pool", "sp")


class Buf:
    _n = 0

    def __init__(self, ap, name="", rng=None):
        self.ap = ap
        self.name = name
        self.key = Buf._n
        Buf._n += 1
        self.rng = rng
        self.dead = False
        self.psum = False

    def __getitem__(self, idx):
        return self.ap[idx]


class Op:
    __slots__ = ("eng", "fn", "deps", "needs_inc", "inc_no", "dma_key", "dma_val", "odeps", "cost", "users", "nrem", "ready",
                 "fin", "pos", "tag", "kind")

    def __init__(self, eng, fn, dma_key):
        self.eng = eng
        self.fn = fn
        self.deps = []
        self.odeps = []
        self.cost = 0.3
        self.users = []
        self.nrem = 0
        self.ready = 0.0
        self.fin = -1.0
        self.pos = 0
        self.needs_inc = False
        self.inc_no = 0
        self.dma_key = dma_key
        self.dma_val = 0


class Prog:
    def __init__(self, nc):
        self.nc = nc
        self.ops = {e: [] for e in ENGS}
        self.lastw = {}
        self.readers = {}
        self.dma_last = {}
        self.dma_cnt = {}
        self.alias = {}
        self.tag = ""

    def add(self, eng, fn, reads=(), writes=(), dma_key=None, extra=(), cost=0.3):
        op = Op(eng, fn, dma_key)
        op.cost = cost
        op.tag = self.tag
        op.kind = "%s:%.2f" % (getattr(fn, "__qualname__", "?").split(".")[1] if "." in getattr(fn, "__qualname__", "") else "?", cost)
        deps = {}
        for b in reads:
            assert not b.dead, b.name
            w = self.lastw.get(b.key)
            if w is not None:
                deps[id(w)] = (w, "raw")
            if b.psum:
                for r in self.readers.get(b.key, ()):
                    if r.eng != eng and id(r) not in deps:
                        deps[id(r)] = (r, "raw")
        for b in writes:
            assert not b.dead, b.name
            w = self.lastw.get(b.key)
            if w is not None:
                deps[id(w)] = (w, "waw")
            for r in self.readers.get(b.key, ()):
                if id(r) not in deps:
                    deps[id(r)] = (r, "war")
            for r in self.alias.pop(b.key, ()):
                deps[id(r)] = (r, "waw")
        for x in extra:
            deps[id(x)] = (x, "raw")
        if dma_key is not None:
            p = self.dma_last.get(dma_key)
            if p is not None:
                deps[id(p)] = (p, "dma")
            self.dma_last[dma_key] = op
            self.dma_cnt[dma_key] = self.dma_cnt.get(dma_key, 0) + 1
            op.dma_val = 16 * self.dma_cnt[dma_key]
        for d, kind in deps.values():
            if d is op:
                continue
            if d.dma_key is None and d.eng == eng and dma_key is None:
                if eng == "pe":
                    op.odeps.append(d)
                    continue
            op.deps.append(d)
            if d.dma_key is None:
                d.needs_inc = True
        for b in reads:
            self.readers.setdefault(b.key, []).append(op)
        for b in writes:
            self.lastw[b.key] = op
            self.readers[b.key] = []
        self.ops[eng].append(op)
        return op

    def retarget(self, old_bufs, new_buf):
        pend = []
        for b in old_bufs:
            w = self.lastw.get(b.key)
            if w is not None:
                pend.append(w)
            pend.extend(self.readers.get(b.key, ()))
            b.dead = True
        self.alias.setdefault(new_buf.key, []).extend(pend)

    def schedule(self, window):
        LAT = LATENCY
        allops = []
        for e in ENGS:
            for i, op in enumerate(self.ops[e]):
                op.pos = i
                allops.append(op)
        for op in allops:
            ds = {id(d): d for d in op.deps}
            for d in op.odeps:
                ds[id(d)] = d
            op.nrem = len(ds)
            for d in ds.values():
                d.users.append(op)
        queues = {e: list(self.ops[e]) for e in ENGS}
        heads = {e: 0 for e in ENGS}
        done = {e: [False] * len(queues[e]) for e in ENGS}
        free = {e: 0.0 for e in ENGS}
        order = {e: [] for e in ENGS}
        ntot = len(allops)
        nsch = 0
        while nsch < ntot:
            best = None
            for e in ENGS:
                q = queues[e]
                h = heads[e]
                dn = done[e]
                n = len(q)
                if h >= n:
                    continue
                w = window.get(e, 1)
                fe = free[e]
                cnt = 0
                i = h
                while i < n and cnt < w:
                    if not dn[i]:
                        cnt += 1
                        op = q[i]
                        if op.nrem == 0:
                            st = op.ready if op.ready > fe else fe
                            if best is None or st < best[0] - 1e-9:
                                best = (st, e, i, op)
                            if st <= fe:
                                break
                    i += 1
            st, e, i, op = best
            op.fin = st + op.cost
            free[e] = op.fin if op.dma_key is None else st + 0.05
            done[e][i] = True
            order[e].append(op)
            while heads[e] < len(queues[e]) and done[e][heads[e]]:
                heads[e] += 1
            fin = op.fin + LAT
            for u in op.users:
                u.nrem -= 1
                if fin > u.ready:
                    u.ready = fin
            nsch += 1
        self.ops = order
        return max(free.values())

    def emit(self, stack, final_wait_keys=()):
        nc = self.nc
        esem = {e: stack.enter_context(nc.semaphore("s_" + e)) for e in ENGS}
        dsem = {}
        for i, k in enumerate(self.dma_cnt):
            dsem[k] = stack.enter_context(nc.semaphore("d%d" % i))
        for e in ENGS:
            n = 0
            for op in self.ops[e]:
                if op.dma_key is None and op.needs_inc:
                    n += 1
                    op.inc_no = n
        block = stack.enter_context(nc.Block())

        def run(ename, eh):
            seen = {}
            for op in self.ops[ename]:
                for d in op.deps:
                    if d.dma_key is not None:
                        s, v = dsem[d.dma_key], d.dma_val
                    else:
                        s, v = esem[d.eng], d.inc_no
                    sid = id(s)
                    if seen.get(sid, 0) >= v:
                        continue
                    seen[sid] = v
                    eh.wait_ge(s, v)
                ins = op.fn(eh)
                if op.dma_key is not None:
                    ins.then_inc(dsem[op.dma_key], 16)
                elif op.needs_inc:
                    ins.then_inc(esem[ename], 1)
            if ename == "sp":
                for k in final_wait_keys:
                    eh.wait_ge(dsem[k], 16 * self.dma_cnt[k])

        @block.tensor
        def _(e):
            run("pe", e)

        @block.scalar
        def _(e):
            run("act", e)

        @block.vector
        def _(e):
            run("dve", e)

        @block.gpsimd
        def _(e):
            run("pool", e)

        @block.sync
        def _(e):
            run("sp", e)


class Builder:
    def __init__(self, n_main=8, n_pre=8, do_sample=True, do_convert=True):
        self.n_main, self.n_pre, self.do_sample = n_main, n_pre, do_sample
        self.do_convert = do_convert
        self.nc = nc = bass.Bass("TRN2", target_bir_lowering=False)
        self.st = ExitStack()
        self.P = Prog(nc)
        self.out_keys = []
        self.live = []
        self.zomb = []
        di = lambda n, s: nc.dram_tensor(n, s, F32, kind="ExternalInput").ap()
        do = lambda n, s: nc.dram_tensor(n, s, F32, kind="ExternalOutput").ap()
        nm, npre = max(n_main, 1) * 512, max(n_pre, 1) * 512
        self.xm = di("xm", [nm, D]); self.xpre = di("xpre", [npre, D]); self.flag_in = di("flag", [128, 1])
        self.xs_in = di("xs", [64, D]); self.sssm = di("sssm", [4, NH, 64, 128]); self.sconv = di("sconv", [4, 3, CD])
        self.w = {}
        for n, s in (("norm_mix_g", [D]), ("w_in", [D, DIN]), ("sgu_ln_g", [D]), ("sgu_ln_b", [D]), ("sgu_w", [NG, 128, 128]),
                     ("sgu_b", [NG, 128]), ("w_a", [D, D]), ("conv_w", [4, CD]), ("conv_b", [CD]), ("dt_bias", [NH]),
                     ("a_log", [NH]), ("d_skip", [NH]), ("ssm_norm_g", [DS]), ("w_b", [DS, D]), ("w_o", [D, D]),
                     ("norm_ffn_g", [D]), ("w_gate", [D, DFF]), ("w_up", [D, DFF]), ("w_down", [DFF, D]), ("norm_final_g", [D])):
            self.w[n] = di(n, s)
        self.yp = do("yp", [nm, D]); self.ys = do("ys", [64, D]); self.ssm_p = do("ssm_p", [NH, 64, 128])
        self.conv_p = do("conv_p", [3, CD]); self.ssm_s = do("ssm_s", [4, NH, 64, 128]); self.conv_s = do("conv_s", [4, 3, CD])
        self.v_s = do("v_s", [64, D])
        self.wbf = {}
        for n, s in (("w_in", [D, DIN]), ("w_a", [D, D]), ("w_b", [DS, D]), ("w_o", [D, D]), ("w_gate", [D, DFF]),
                     ("w_up", [D, DFF]), ("w_down", [DFF, D])):
            self.wbf[n] = nc.dram_tensor(n + "_bf", s, BF16, kind="Internal").ap()
        self.cdone = {}
        self.cv_pending = []
        self.cv_bg = False
        self.cv_rate = 0
        self.cv_k = {0: 0, 1: 0}
        self.wcount = 0

    def sb(self, name, shape, dt=F32):
        return Buf(self.st.enter_context(self.nc.sbuf_tensor(name, shape, dt)), name)

    def carve(self, name, off, shape, dt):
        esz = 2 if dt == BF16 else 4
        n = 1
        for d in shape[1:]:
            n *= d
        nb = n * esz
        assert off % 4 == 0 and off + nb <= self.ARENA, (name, off, nb)
        ap = self.arena[:, off:off + nb].bitcast(dt)
        if len(shape) == 3:
            ap = ap.rearrange("p (a b) -> p a b", a=shape[1])
        elif len(shape) == 4:
            ap = ap.rearrange("p (a b c) -> p a b c", a=shape[1], b=shape[2])
        b = Buf(ap, name, (off, off + nb))
        lo, hi = off, off + nb
        old = [o for o in self.live if o.rng[0] < hi and lo < o.rng[1]]
        pend = []
        for o in old:
            w = self.P.lastw.get(o.key)
            p = ([w] if w is not None else []) + list(self.P.readers.get(o.key, ())) + list(self.P.alias.get(o.key, ()))
            o.dead = True
            self.zomb.append((o.rng, p))
        self.live = [o for o in self.live if not o.dead]
        for rng, p in self.zomb:
            if rng[0] < hi and lo < rng[1]:
                pend.extend(p)
        if pend:
            self.P.alias.setdefault(b.key, []).extend(pend)
        self.live.append(b)
        return b

    @staticmethod
    def nfree(ap):
        n = 1
        for d in ap.shape[1:]:
            n *= d
        return n

    def ecost(self, eng, out):
        n = self.nfree(out)
        if eng == "act":
            return 0.26 + n * 0.00085
        if eng == "dve":
            return 0.12 + n * 0.00105
        return 0.35 + n * 0.0021

    def mm(self, out, lhsT, rhs, start, stop, reads, writes):
        n = self.nfree(rhs)
        c = max(n, 64) / PE_MHZ * (4.0 if rhs.dtype == F32 else 1.0) + 0.012
        return self.P.add("pe", lambda e: e.matmul(out, lhsT=lhsT, rhs=rhs, start=start, stop=stop), reads, writes, cost=c)

    def tr(self, out, in_, ident, reads, writes):
        c = 0.09 * (4.0 if in_.dtype == F32 else 1.0)
        return self.P.add("pe", lambda e: e.transpose(out=out, in_=in_, identity=ident), reads, writes, cost=c)

    def act(self, out, in_, func, reads, writes, **kw):
        return self.P.add("act", lambda e: e.activation(out=out, in_=in_, func=func, **kw), reads, writes, cost=self.ecost("act", out))

    def tt(self, eng, out, in0, in1, op, reads, writes):
        return self.P.add(eng, lambda e: e.tensor_tensor(out=out, in0=in0, in1=in1, op=op), reads, writes, cost=self.ecost(eng, out))

    def ts(self, eng, out, in0, s1, s2, op0, op1, reads, writes):
        c = self.ecost(eng, out)
        if s2 is None:
            return self.P.add(eng, lambda e: e.tensor_scalar(out=out, in0=in0, scalar1=s1, scalar2=None, op0=op0), reads, writes, cost=c)
        return self.P.add(eng, lambda e: e.tensor_scalar(out=out, in0=in0, scalar1=s1, scalar2=s2, op0=op0, op1=op1), reads, writes, cost=c)

    def stt(self, out, in0, scalar, in1, op0, op1, reads, writes):
        return self.P.add("dve", lambda e: e.scalar_tensor_tensor(out=out, in0=in0, scalar=scalar, in1=in1, op0=op0, op1=op1), reads, writes,
                          cost=self.ecost("dve", out))

    def cp(self, eng, out, in_, reads, writes):
        if eng == "act":
            return self.act(out, in_, AF.Copy, reads, writes)
        return self.P.add(eng, lambda e: e.tensor_copy(out=out, in_=in_), reads, writes, cost=self.ecost(eng, out))

    def red(self, out, in_, reads, writes):
        return self.P.add("dve", lambda e: e.tensor_reduce(out=out, in_=in_, axis=mybir.AxisListType.X, op=ALU.add), reads, writes)

    def ms(self, eng, ap, val, writes):
        return self.P.add(eng, lambda e: e.memset(ap, val), (), writes)

    def dma(self, q, out, in_, reads, writes, key, extra=(), is_out=False):
        if is_out:
            self.out_keys.append(key)
        c = 2.0 + self.nfree(out) * out.shape[0] * 4 / 200e3
        return self.P.add(q, lambda e: e.dma_start(out=out, in_=in_), reads, writes, dma_key=key, extra=extra, cost=c)

    def rsqrt(self, out_buf, out, in_buf, in_, scale, npart, ncols):
        self.ts("dve", out, in_, scale, EPS, ALU.mult, ALU.add, [in_buf], [out_buf])
        self.tt("pool", out, out, self.mhalf[0:npart, 0:ncols], ALU.pow, [out_buf, self.mhalf], [out_buf])

    def wload(self, name, row0, kc, col0, width):
        i = self.wcount % 3
        self.wcount += 1
        slot = self.wslot[i]
        src = self.wbf[name][row0:row0 + kc * 128, col0:col0 + width].rearrange("(kc p) n -> p kc n", p=128)
        key = name
        if name == "w_in":
            key = "w_in_pre" if C_XS <= col0 < C_GA else "w_in_rest"
        assert not [j for j in self.cv_pending if j[1] == key], key
        extra = list(self.cdone.get(key, {}).values())
        self.P.add("sp", lambda e: e.dma_start(out=slot[:, 0:kc, 0:width], in_=src), (), [slot], dma_key=("w", i), extra=extra,
                   cost=2.0 + kc * width * 256 / 250e3)
        if self.cv_bg:
            for _ in range(self.cv_rate):
                if self.cv_pending:
                    self.cv_job(self.cv_pending.pop(0), 1)
        return slot

    def cv_job(self, job, mode):
        name, key, rb, c0, cw = job
        slots = self.cv_slots[mode]
        k = self.cv_k[mode]
        self.cv_k[mode] += 1
        si = k % len(slots)
        f, b = slots[si]
        src, dst = self.w[name], self.wbf[name]
        self.dma("sp", f[:, 0:cw], src[rb * 128:(rb + 1) * 128, c0:c0 + cw], [], [f], ("cvl", mode, si))
        if mode == 0:
            self.cp("dve" if k % 2 == 0 else "act", b[:, 0:cw], f[:, 0:cw], [f], [b])
            op = self.dma("pool", dst[rb * 128:(rb + 1) * 128, c0:c0 + cw], b[:, 0:cw], [b], [], ("cvs", mode, si))
        else:
            self.cp("act", b[:, 0:cw], f[:, 0:cw], [f], [b])
            op = self.dma("act", dst[rb * 128:(rb + 1) * 128, c0:c0 + cw], b[:, 0:cw], [b], [], ("cvs", mode, si))
        self.cdone.setdefault(key, {})[(mode, si)] = op

    def featmajor(self, rows, dst_buf, dst_ap, stage_buf, n, tag):
        R = len(rows)
        for r, row in enumerate(rows):
            self.dma("pool", stage_buf[r:r + 1, 0:n * 128], row.rearrange("(o n) -> o n", o=1), [], [stage_buf], ("fm", tag, r))
        per = 512 // R
        j0 = 0
        while j0 < n:
            jn = min(per, n - j0)
            pb = self.psb[0]
            for j in range(jn):
                self.tr(pb[:, j * R:(j + 1) * R], stage_buf[0:R, (j0 + j) * 128:(j0 + j + 1) * 128], self.identf[0:R, 0:R],
                        [stage_buf, self.identf], [pb])
            self.cp("dve", dst_ap[:, j0:j0 + jn, :], pb[:, 0:jn * R].rearrange("p (a b) -> p a b", a=jn), [pb], [dst_buf])
            j0 += jn

    def build(self):
        nc, P = self.nc, self.P
        self.ARENA = 130 * 1024
        self.arena = self.st.enter_context(nc.sbuf_tensor("arena", [128, self.ARENA], U8))
        sb = self.sb
        self.wslot = [sb("wslot%d" % i, [128, 16, 512], BF16) for i in range(3)]
        self.S = [sb("S%d" % g, [128, 512], F32) for g in range(NG)]
        self.identf = sb("identf", [128, 128]); self.ident = sb("ident", [128, 128], BF16)
        self.trif = sb("trif", [128, 128]); self.trib = sb("trib", [128, 128], BF16)
        self.negb = sb("negb", [128, 128], BF16); self.onesf = sb("onesf", [128, 128]); self.onesb = sb("onesb", [128, 128], BF16)
        self.wmaskT = sb("wmaskT", [128, NG, 128], BF16); self.b2 = sb("b2", [64, NG * 128], BF16)
        self.gmixT = sb("gmixT", [128, 16, 1]); self.gffnT = sb("gffnT", [128, 16, 1]); self.gssmT = sb("gssmT", [128, 32, 1])
        self.convw = sb("convw", [128, 48, 5])
        self.dtb = sb("dtb", [128, NH]); self.abc = sb("abc", [128, NH]); self.dsk = sb("dsk", [128, NH])
        self.flag = sb("flagt", [128, 1]); self.mhalf = sb("mhalf", [128, 8]); self.halo = sb("halo", [128, 48, 3])
        self.ps = [Buf(self.st.enter_context(nc.psum_tensor("ps%d" % i, [128, 512], F32)), "ps%d" % i) for i in range(8)]
        for b in self.ps:
            b.psum = True
        self.psb = self.ps
        self.pSm_sub = [Buf(self.ps[4].ap, "pSm_%d" % i) for i in range(3)]
        self.bulk_i = 0
        self.bulk_banks = (0, 1)
        self.consts()
        if self.do_convert:
            self.convert_weights()
        for g in range(NG):
            self.ms("dve", self.S[g][:], 0.0, [self.S[g]])
        self.ms("dve", self.halo[:], 0.0, [self.halo])
        self.cv_bg = True
        for t in range(self.n_pre):
            self.tile("pre", self.xpre[t * 512:(t + 1) * 512, :], None, t)
        self.cv_flush()
        if self.n_pre > 0:
            for g in range(NG):
                self.ts("dve", self.S[g][:], self.S[g][:], self.flag[:, 0:1], None, ALU.mult, None, [self.S[g], self.flag], [self.S[g]])
            self.ts("dve", self.halo[:], self.halo[:], self.flag[:, 0:1], None, ALU.mult, None, [self.halo, self.flag], [self.halo])
        for t in range(self.n_main):
            self.tile("main", self.xm[t * 512:(t + 1) * 512, :], self.yp[t * 512:(t + 1) * 512, :], t)
        if self.n_main > 0:
            self.store_prompt_state()
        if self.do_sample:
            self.tile("sample", self.xs_in, self.ys, 0)
        if SCHED:
            self.est = P.schedule({"pe": WIN[0], "act": WIN[1], "dve": WIN[2], "pool": WIN[3], "sp": 1})
        P.emit(self.st, final_wait_keys=list(dict.fromkeys(self.out_keys)))
        self.st.close()
        return nc

    def consts(self):
        P = self.P
        self.ms("pool", self.identf[:], 1.0, [self.identf])
        P.add("pool", lambda e: e.affine_select(out=self.identf[:], in_=self.identf[:], pattern=[[-1, 128]], compare_op=ALU.is_equal,
                                                fill=0.0, base=0, channel_multiplier=1), [self.identf], [self.identf])
        self.ms("pool", self.trif[:], 1.0, [self.trif])
        P.add("pool", lambda e: e.affine_select(out=self.trif[:], in_=self.trif[:], pattern=[[1, 128]], compare_op=ALU.is_ge,
                                                fill=0.0, base=0, channel_multiplier=-1), [self.trif], [self.trif])
        negf = self.carve("negf", 0, [128, 128], F32)
        self.ms("pool", negf[:], -30000.0, [negf])
        P.add("pool", lambda e: e.affine_select(out=negf[:], in_=negf[:], pattern=[[-1, 128]], compare_op=ALU.is_gt,
                                                fill=0.0, base=0, channel_multiplier=1), [negf], [negf])
        self.ms("pool", self.onesf[:], 1.0, [self.onesf])
        self.ms("pool", self.mhalf[:], -0.5, [self.mhalf])
        self.cp("dve", self.ident[:], self.identf[:], [self.identf], [self.ident])
        self.cp("dve", self.trib[:], self.trif[:], [self.trif], [self.trib])
        self.cp("dve", self.negb[:], negf[:], [negf], [self.negb])
        self.cp("dve", self.onesb[:], self.onesf[:], [self.onesf], [self.onesb])
        w = self.w
        stage = self.carve("cstage", 1024, [128, CD], F32)
        self.featmajor([w["norm_mix_g"]], self.gmixT, self.gmixT[:], stage, 16, "gm")
        self.featmajor([w["norm_ffn_g"]], self.gffnT, self.gffnT[:], stage, 16, "gf")
        self.featmajor([w["ssm_norm_g"]], self.gssmT, self.gssmT[:], stage, 32, "gs")
        self.featmajor([w["conv_w"][k, :] for k in range(4)] + [w["conv_b"]], self.convw, self.convw[:], stage, 48, "cw")
        self.dma("pool", self.dtb[:], w["dt_bias"].partition_broadcast(128), [], [self.dtb], "dtb")
        self.dma("pool", self.abc[:], w["a_log"].partition_broadcast(128), [], [self.abc], "abc")
        self.dma("pool", self.dsk[:], w["d_skip"].partition_broadcast(128), [], [self.dsk], "dsk")
        self.dma("pool", self.flag[:], self.flag_in[:, :], [], [self.flag], "flag")
        self.act(self.abc[:], self.abc[:], AF.Exp, [self.abc], [self.abc])
        self.ts("dve", self.abc[:], self.abc[:], -1.0, None, ALU.mult, None, [self.abc], [self.abc])
        wst = self.carve("wst", 32 * 1024, [128, NG, 128], F32)
        self.dma("pool", wst[:], w["sgu_w"].rearrange("g t s -> t g s"), [], [wst], "wst")
        for g in range(NG):
            pb = self.ps[g % 2]
            self.tr(pb[:, 0:128], wst[:, g, :], self.identf[:], [wst, self.identf], [pb])
            self.tt("dve", self.wmaskT[:, g, :], pb[:, 0:128], self.trif[:], ALU.mult, [pb, self.trif], [self.wmaskT])
        bst = self.carve("bst", 40 * 1024, [64, NG * 128], F32)
        bhi = self.carve("bhi", 48 * 1024, [64, NG * 128], BF16)
        self.ms("dve", self.b2[:], 0.0, [self.b2])
        brow = w["sgu_b"].rearrange("(o g) t -> o (g t)", o=1)
        self.dma("pool", bst[0:1, :], brow, [], [bst], "bst0")
        self.dma("pool", bst[32:33, :], brow, [], [bst], "bst1")
        self.cp("dve", self.b2[0:1, :], bst[0:1, :], [bst], [self.b2])
        self.cp("dve", bhi[32:33, :], bst[32:33, :], [bst], [bhi])
        self.tt("dve", self.b2[32:33, :], bst[32:33, :], bhi[32:33, :], ALU.subtract, [bst, bhi], [self.b2])

    def convert_weights(self):
        jobs0, jobs1 = [], []

        def addm(lst, name, key, c_lo, c_hi, cwmax):
            for rb in range(self.w[name].shape[0] // 128):
                for c0 in range(c_lo, c_hi, cwmax):
                    lst.append((name, key, rb, c0, min(cwmax, c_hi - c0)))

        addm(jobs0, "w_in", "w_in_pre", C_XS, C_GA, 3104)
        CWS = 1792
        addm(jobs1, "w_in", "w_in_rest", 0, C_XS, CWS)
        addm(jobs1, "w_in", "w_in_rest", C_GA, DIN, CWS)
        for n in ("w_a", "w_b", "w_o", "w_gate", "w_up", "w_down"):
            addm(jobs1, n, n, 0, self.w[n].shape[1], CWS)
        big = []
        for i in range(3):
            f = self.carve("cvf%d" % i, i * 24576, [128, 4096], F32)
            b = self.carve("cvb%d" % i, i * 24576 + 16384, [128, 4096], BF16)
            big.append((f, b))
        self.cv_slots = {0: big}
        for j in jobs0:
            self.cv_job(j, 0)
        if self.n_pre == 0:
            for j in jobs1:
                self.cv_job(j, 0)
        else:
            small = []
            for i, off in enumerate((97 * 1024, 97 * 1024 + 10752, 97 * 1024 + 21504, 70 * 1024, 86 * 1024)):
                f = self.carve("cwf%d" % i, off, [128, CWS], F32)
                b = self.carve("cwb%d" % i, off + CWS * 4, [128, CWS], BF16)
                small.append((f, b))
            self.cv_slots[1] = small
            self.cv_pending = jobs1
            self.cv_rate = -(-len(jobs1) // (13 * self.n_pre - 2))

    def cv_flush(self):
        while self.cv_pending:
            self.cv_job(self.cv_pending.pop(0), 1)
        self.cv_bg = False

    def norm_to_fm(self, xt, s, L, gT, h_fm, xb, stat, pbs):
        x = xt[s]
        self.ms("dve", stat[0:L, 0:1], 0.0, [stat])
        self.act(xb[0:L, :], x[0:L, :], AF.Square, [x], [xb, stat], accum_out=stat[0:L, 0:1])
        self.rsqrt(stat, stat[0:L, 1:2], stat, stat[0:L, 0:1], 1.0 / D, L, 1)
        self.ts("dve", xb[0:L, :], x[0:L, :], stat[0:L, 1:2], None, ALU.mult, None, [x, stat], [xb])
        for half in range(2):
            pb = pbs[half]
            pv = pb[:].bitcast(BF16)
            for j in range(8):
                kc = half * 8 + j
                self.tr(pv[:, j * L:(j + 1) * L], xb[0:L, kc * 128:(kc + 1) * 128], self.ident[0:L, 0:L], [xb, self.ident], [pb])
            self.tt("dve", h_fm[:, half * 8:half * 8 + 8, s * L:(s + 1) * L], pv[:, 0:8 * L].rearrange("p (a b) -> p a b", a=8),
                    gT[:, half * 8:half * 8 + 8, :].to_broadcast([128, 8, L]), ALU.mult, [pb, gT], [h_fm])

    def bulk(self):
        banks = self.bulk_banks
        self.bulk_i = (self.bulk_i + 1) % len(banks)
        return self.ps[banks[self.bulk_i]]

    def tile(self, mode, xsrc, ydst, tidx):
        self.P.tag = "%s%d" % (mode, tidx)
        sample = mode == "sample"
        pre = mode == "pre"
        L = 16 if sample else 128
        NS = 4
        T = L * NS
        NSEG, LSEG = (4, 16) if sample else (1, 512)
        K = 1024
        cv = self.carve
        ps = self.ps
        ident, identf = self.ident, self.identf
        L_ssd, NS_ssd = L, NS
        if sample:
            L, NS = 64, 1
        h_fm = cv("h_fm", 0, [128, 16, T], BF16)
        xt = [cv("xt%d" % s, 16 * K + s * 8 * K, [128, D], F32) for s in range(NS)]
        xb = [cv("xb%d" % i, 48 * K + i * 4 * K, [128, D], BF16) for i in range(2)]
        stat = [cv("stat%d" % i, 56 * K + i * 64, [128, 4], F32) for i in range(2)]
        for s in range(NS):
            self.dma("pool", xt[s][0:L, :], xsrc[s * L:(s + 1) * L, :], [], [xt[s]], ("xt", s))
        for s in range(NS):
            self.norm_to_fm(xt, s, L, self.gmixT, h_fm, xb[s % 2], stat[s % 2], (ps[0], ps[1]))
        L, NS = L_ssd, NS_ssd
        if STOP <= 1:
            return
        self.P.tag = "%s%d.p2" % (mode, tidx)
        self.bulk_banks = (0, 1)
        o = 16 * K
        dtr = cv("dtr", o, [128, NS, NH], F32); dt = cv("dt", o + K, [128, NS, NH], F32); da = cv("da", o + 2 * K, [128, NS, NH], F32)
        negcum = cv("negcum", o + 3 * K, [128, NS, NH], F32); ecum = cv("ecum", o + 4 * K, [128, NS, NH], F32)
        edec = cv("edec", o + 5 * K, [128, NS, NH], F32); dtt = cv("dtt", o + 6 * K, [128, NS, NH], F32)
        dahi = cv("dahi", o + 7 * K, [128, NS, NH], BF16); dalo = cv("dalo", o + 7 * K + 512, [128, NS, NH], BF16)
        BC = cv("BC", 24 * K, [128, 16, T], BF16)
        xraw = [cv("xraw%d" % i, 40 * K + i * 2304, [128, NSEG, 3 + LSEG], F32) for i in range(2)]
        cacc = [cv("cacc%d" % i, 45 * K + i * 2 * K, [128, NSEG, LSEG], F32) for i in range(2)]
        xsfm = [cv("xsfm%d" % i, 49 * K + i * 4 * K, [128, 4, T], BF16) for i in range(2)]
        zs = [cv("zs%d" % i, 57 * K + i * 4 * K, [128, NS, 512], BF16) for i in range(2)]
        if sample:
            sconvT = cv("sconvT", 28 * K, [128, 48, 12], F32)
            convo = cv("convo", 31 * K, [128, 48, 12], F32)
            sst = [cv("sst%d" % i, 34 * K + i * 2 * K, [128, 4, 128], F32) for i in range(2)]
        if not pre:
            ynfm = cv("ynfm", 97 * K, [128, 32, T], BF16)
        Wd = self.wload("w_in", 0, 16, C_DT, 64)
        pdt, pcum, pcl = ps[2], ps[3], ps[7]
        for s in range(NS):
            for kc in range(16):
                self.mm(pdt[0:L, s * NH:(s + 1) * NH], h_fm[:, kc, s * L:(s + 1) * L], Wd[:, kc, 0:NH], kc == 0, kc == 15, [h_fm, Wd], [pdt])
        if STOP <= 1.2:
            return
        self.tt("dve", dtr[0:L], pdt[0:L, 0:NS * NH].rearrange("p (a b) -> p a b", a=NS),
                self.dtb[0:L, :].unsqueeze(1).to_broadcast([L, NS, NH]), ALU.add, [pdt, self.dtb], [dtr])
        self.act(dtr[0:L], dtr[0:L], AF.Exp, [dtr], [dtr])
        self.act(dt[0:L], dtr[0:L], AF.Ln, [dtr], [dt], bias=1.0, scale=1.0)
        self.tt("dve", da[0:L], dt[0:L], self.abc[0:L, :].unsqueeze(1).to_broadcast([L, NS, NH]), ALU.mult, [dt, self.abc], [da])
        self.act(dtr[0:L], dt[0:L], AF.Ln, [dt], [dtr])
        self.cp("dve", dahi[0:L], da[0:L], [da], [dahi])
        self.tt("dve", dalo[0:L], da[0:L], dahi[0:L], ALU.subtract, [da, dahi], [dalo])
        if STOP <= 1.4:
            return
        for s in range(NS):
            self.mm(pcum[0:L, s * NH:(s + 1) * NH], self.trif[0:L, 0:L], da[0:L, s, :], True, True, [self.trif, da], [pcum])
            self.mm(pcl[:, s * NH:(s + 1) * NH], self.onesf[0:L, :], da[0:L, s, :], True, True, [self.onesf, da], [pcl])
        if STOP <= 1.6:
            return
        c3 = lambda pb, n: pb[0:n, 0:NS * NH].rearrange("p (a b) -> p a b", a=NS)
        self.ts("dve", negcum[0:L], c3(pcum, L), -1.0, None, ALU.mult, None, [pcum], [negcum])
        self.tt("dve", dtr[0:L], dtr[0:L], negcum[0:L], ALU.add, [dtr, negcum], [dtr])
        self.act(ecum[0:L], c3(pcum, L), AF.Exp, [pcum], [ecum])
        self.act(edec[:], c3(pcl, 128), AF.Exp, [pcl], [edec])
        self.tt("dve", dtt[0:L], c3(pcl, L), negcum[0:L], ALU.add, [pcl, negcum], [dtt])
        self.act(dtt[0:L], dtt[0:L], AF.Exp, [dtt], [dtt])
        self.tt("dve", dtt[0:L], dtt[0:L], dt[0:L], ALU.mult, [dtt, dt], [dtt])
        if STOP <= 2:
            return
        if sample:
            stg = cv("cstg", 65 * K, [16, CD], F32)
            for q in range(4):
                rows = [self.sconv[q, k, :] for k in range(3)]
                for r, row in enumerate(rows):
                    self.dma("pool", stg[r:r + 1, :], row.rearrange("(o n) -> o n", o=1), [], [stg], ("fm", "sc", r))
                j0 = 0
                while j0 < 48:
                    jn = min(128, 48 - j0)
                    pb = ps[0]
                    for j in range(jn):
                        self.tr(pb[:, j * 3:(j + 1) * 3], stg[0:3, (j0 + j) * 128:(j0 + j + 1) * 128], identf[0:3, 0:3], [stg, identf], [pb])
                    self.cp("dve", sconvT[:, j0:j0 + jn, q * 3:(q + 1) * 3], pb[:, 0:jn * 3].rearrange("p (a b) -> p a b", a=jn), [pb], [sconvT])
                    j0 += jn
        self.cv_i = 0

        def conv_chunk(W, j, c, out_ap, out_buf):
            i = self.cv_i = self.cv_i ^ 1
            pb = self.bulk()
            for kc in range(16):
                self.mm(pb[:, 0:T], W[:, kc, j * 128:(j + 1) * 128], h_fm[:, kc, :], kc == 0, kc == 15, [W, h_fm], [pb])
            xr, ca = xraw[i], cacc[i]
            if sample:
                self.cp("pool", xr[:, :, 0:3], sconvT[:, c, :].rearrange("p (a b) -> p a b", a=4), [sconvT], [xr])
            else:
                self.cp("pool", xr[:, 0, 0:3], self.halo[:, c, :], [self.halo], [xr])
            self.cp("act", xr[:, :, 3:3 + LSEG], pb[:, 0:T].rearrange("p (a b) -> p a b", a=NSEG), [pb], [xr])
            if sample:
                self.cp("pool", convo[:, c, :].rearrange("p (a b) -> p a b", a=4), xr[:, :, LSEG:LSEG + 3], [xr], [convo])
            else:
                self.cp("pool", self.halo[:, c, :], xr[:, 0, LSEG:LSEG + 3], [xr], [self.halo])
            cw = self.convw
            self.ts("dve", ca[:], xr[:, :, 0:LSEG], cw[:, c, 0:1], cw[:, c, 4:5], ALU.mult, ALU.add, [xr, cw], [ca])
            for k in range(1, 4):
                self.stt(ca[:], xr[:, :, k:k + LSEG], cw[:, c, k:k + 1], ca[:], ALU.mult, ALU.add, [xr, cw, ca], [ca])
            self.act(out_ap, ca[:].rearrange("p a b -> p (a b)"), AF.Silu, [ca], [out_buf])

        for wt in range(2 if (pre and tidx < self.n_pre - 1) else 4):
            W = self.wload("w_in", 0, 16, C_B + wt * 512, 512)
            for j in range(4):
                conv_chunk(W, j, 32 + wt * 4 + j, BC[:, wt * 4 + j, :], BC)
        if STOP <= 3:
            return
        tb = 65 * K
        tmp = []
        for i in range(2):
            b0 = tb + i * 16 * K
            d = dict(xs_tm=cv("xs_tm%d" % i, b0, [128, 512], BF16), xw=cv("xw%d" % i, b0 + 2 * K, [128, 512], BF16),
                     Btm=cv("Btm%d" % i, b0 + 4 * K, [128, 128], BF16))
            if not pre:
                d.update(
                    xsD=cv("xsD%d" % i, b0 + 3 * K, [128, 512], BF16), cbt=cv("cbt%d" % i, b0 + 4 * K + 256, [128, 128], BF16),
                    st=cv("sst%d_" % i, b0 + 4 * K + 768, [128, 4], F32),
                    Dm=cv("Dm%d" % i, b0 + 5 * K, [128, 8, 128], BF16), M=cv("M%d" % i, b0 + 9 * K, [128, 8, 128], BF16),
                    t1=cv("t1%d" % i, b0 + 11 * K, [128, 512], F32), junk=cv("junk%d" % i, b0 + 13 * K, [128, 512], BF16),
                    yn=cv("yn%d" % i, b0 + 14 * K, [128, 512], BF16), Sbf=cv("Sbf%d" % i, b0 + 15 * K, [128, 512], BF16))
            tmp.append(d)
        pD = (ps[2], ps[3]); pSm = ps[4]; pY = ps[5]; pZ = ps[6]; pU = ps[7]
        pSm_x = pSm_b = pSm_c = pSm
        pvx = pSm[:].bitcast(BF16)
        step = 0
        wh = {}

        def bulk_unit(g, u):
            xf_, zz_ = xsfm[g % 2], zs[g % 2]
            if u < 4:
                if u == 0:
                    wh["x", g] = self.wload("w_in", 0, 16, C_XS + g * 512, 512)
                conv_chunk(wh["x", g], u, 4 * g + u, xf_[:, u, :], xf_)
            else:
                s_ = u - 4
                if s_ == 0:
                    wh["z", g] = self.wload("w_in", 0, 16, C_Z + g * 512, 512)
                Wz = wh["z", g]
                pb = self.bulk()
                for kc in range(16):
                    self.mm(pb[0:L, :], h_fm[:, kc, s_ * L:(s_ + 1) * L], Wz[:, kc, :], kc == 0, kc == 15, [h_fm, Wz], [pb])
                self.act(zz_[0:L, s_, :], pb[0:L, :], AF.Silu, [pb], [zz_])

        uorder = [0, 1, 2, 3] if pre else [0, 4, 1, 5, 2, 6, 3, 7]
        upstep = len(uorder) // NS
        for u in uorder:
            bulk_unit(0, u)
        for g in range(NG):
            xf = xsfm[g % 2]
            zz = zs[g % 2]
            Sg = self.S[g]
            hs = slice(8 * g, 8 * g + 8)
            for s in range(NS):
                if g + 1 < NG:
                    for u in uorder[s * upstep:(s + 1) * upstep]:
                        bulk_unit(g + 1, u)
                d = tmp[step % 2]
                step += 1
                tok = slice(s * L, (s + 1) * L)
                Sbf = d.get("Sbf")
                if sample:
                    stt_ = sst[s % 2]
                    self.dma("pool", stt_[:], self.sssm[s, 8 * g:8 * g + 8].rearrange("h p n -> (h p) n").rearrange("(j q) n -> q j n", q=128),
                             [], [stt_], ("sst", s % 2))
                    for j in range(4):
                        self.tr(pU[:, j * 128:(j + 1) * 128], stt_[:, j, :], identf[:], [stt_, identf], [pU])
                    self.cp("dve", Sg[:], pU[:], [pU], [Sg])
                if not pre and (s == 0 or sample):
                    self.cp("act", Sbf[:], Sg[:], [Sg], [Sbf])
                for j in range(4):
                    self.tr(pvx[0:L, j * 128:(j + 1) * 128], xf[:, j, tok], ident[:], [xf, ident], [pSm_x])
                xs_tm = d["xs_tm"]
                self.cp("act", xs_tm[0:L, :], pvx[0:L, 0:512], [pSm_x], [xs_tm])
                v3 = lambda b: b[0:L, :].rearrange("p (a b) -> p a b", a=8)
                bc8 = lambda b: b[0:L, s, hs].unsqueeze(2).to_broadcast([L, 8, 64])
                self.tt("dve", v3(d["xw"]), v3(xs_tm), bc8(dtt), ALU.mult, [xs_tm, dtt], [d["xw"]])
                self.tr(pvx[0:L, 512:640], BC[:, g, tok], ident[:], [BC, ident], [pSm_b])
                self.cp("act", d["Btm"][0:L, :], pvx[0:L, 512:640], [pSm_b], [d["Btm"]])
                if not pre:
                    self.tt("pool", v3(d["xsD"]), v3(xs_tm), self.dsk[0:L, hs].unsqueeze(2).to_broadcast([L, 8, 64]), ALU.mult,
                            [xs_tm, self.dsk], [d["xsD"]])
                    self.mm(pSm[0:L, 384:384 + L], BC[:, g, tok], BC[:, 8 + g, tok], True, True, [BC], [pSm_c])
                    self.cp("act", d["cbt"][0:L, 0:L], pSm[0:L, 384:384 + L], [pSm_c], [d["cbt"]])
                    for hh in range(8):
                        h = 8 * g + hh
                        pb = pD[hh // 4]
                        oap = pb[0:L, (hh % 4) * 128:(hh % 4) * 128 + L]
                        self.mm(oap, dahi[0:L, s, h:h + 1].to_broadcast([L, L]), self.trib[0:L, 0:L], True, False, [dahi, self.trib], [pb])
                        self.mm(oap, dalo[0:L, s, h:h + 1].to_broadcast([L, L]), self.trib[0:L, 0:L], False, False, [dalo, self.trib], [pb])
                        self.mm(oap, ident[0:L, 0:L], self.negb[0:L, 0:L], False, True, [ident, self.negb], [pb])
                    Dm, M = d["Dm"], d["M"]
                    for hh in range(8):
                        h = 8 * g + hh
                        pb = pD[hh // 4]
                        self.act(Dm[0:L, hh, 0:L], pb[0:L, (hh % 4) * 128:(hh % 4) * 128 + L], AF.Exp, [pb, dtr], [Dm],
                                 bias=dtr[0:L, s, h:h + 1], scale=1.0)
                    self.tt("dve", M[0:L, :, 0:L], Dm[0:L, :, 0:L], d["cbt"][0:L, 0:L].unsqueeze(1).to_broadcast([L, 8, L]), ALU.mult,
                            [Dm, d["cbt"]], [M])
                    self.mm(pY[0:L, :], ident[0:L, 0:L], d["xsD"][0:L, :], True, False, [ident, d["xsD"]], [pY])
                    for hh in range(8):
                        self.mm(pY[0:L, hh * 64:(hh + 1) * 64], M[0:L, hh, 0:L], xs_tm[0:L, hh * 64:(hh + 1) * 64], False, hh == 7,
                                [M, xs_tm], [pY])
                    self.mm(pZ[0:L, :], BC[:, 8 + g, tok], Sbf[:], True, True, [BC, Sbf], [pZ])
                    t1 = d["t1"]
                    self.tt("dve", v3(t1), pZ[0:L, :].rearrange("p (a b) -> p a b", a=8), bc8(ecum), ALU.mult, [pZ, ecum], [t1])
                    self.tt("dve", t1[0:L, :], t1[0:L, :], pY[0:L, :], ALU.add, [t1, pY], [t1])
                    self.tt("dve", t1[0:L, :], t1[0:L, :], zz[0:L, s, :], ALU.mult, [t1, zz], [t1])
                    st_ = d["st"]
                    self.ms("dve", st_[0:L, 0:1], 0.0, [st_])
                    self.act(d["junk"][0:L, :], t1[0:L, :], AF.Square, [t1], [d["junk"], st_], accum_out=st_[0:L, 0:1])
                    self.rsqrt(st_, st_[0:L, 1:2], st_, st_[0:L, 0:1], 1.0 / 512, L, 1)
                    yn = d["yn"]
                    self.ts("dve", yn[0:L, :], t1[0:L, :], st_[0:L, 1:2], None, ALU.mult, None, [t1, st_], [yn])
                    for j in range(4):
                        self.tr(pvx[:, j * L:(j + 1) * L], yn[0:L, j * 128:(j + 1) * 128], ident[0:L, 0:L], [yn, ident], [pSm_x])
                    self.tt("dve", ynfm[:, 4 * g:4 * g + 4, tok], pvx[:, 0:4 * L].rearrange("p (a b) -> p a b", a=4),
                            self.gssmT[:, 4 * g:4 * g + 4, :].to_broadcast([128, 4, L]), ALU.mult, [pSm_x, self.gssmT], [ynfm])
                self.mm(pU[:, :], d["Btm"][0:L, :], d["xw"][0:L, :], True, True, [d["Btm"], d["xw"]], [pU])
                S3 = Sg[:].rearrange("p (a b) -> p a b", a=8)
                self.tt("dve", S3, S3, edec[:, s, hs].unsqueeze(2).to_broadcast([128, 8, 64]), ALU.mult, [Sg, edec], [Sg])
                self.tt("dve", Sg[:], Sg[:], pU[:, :], ALU.add, [Sg, pU], [Sg])
                if not pre and not sample and s < NS - 1:
                    self.cp("act", tmp[step % 2]["Sbf"][:], Sg[:], [Sg], [tmp[step % 2]["Sbf"]])
                if sample:
                    so = sst[s % 2]
                    for j in range(4):
                        self.tr(pU[:, j * 128:(j + 1) * 128], Sg[:, j * 128:(j + 1) * 128], identf[:], [Sg, identf], [pU])
                    self.cp("dve", so[:], pU[:, :].rearrange("p (a b) -> p a b", a=4), [pU], [so])
                    self.dma("pool", self.ssm_s[s, 8 * g:8 * g + 8].rearrange("h p n -> (h p) n").rearrange("(j q) n -> q j n", q=128),
                             so[:], [so], [], ("sso", s % 2), is_out=True)
        if sample:
            cst = cv("cst", 65 * K, [16, CD], F32)
            for c0 in range(0, 48, 4):
                pb = self.bulk()
                for c in range(c0, c0 + 4):
                    self.tr(pb[0:12, (c - c0) * 128:(c - c0 + 1) * 128], convo[:, c, :], identf[:], [convo, identf], [pb])
                self.cp("dve", cst[0:12, c0 * 128:(c0 + 4) * 128], pb[0:12, :], [pb], [cst])
            self.dma("pool", self.conv_s.rearrange("q k c -> (q k) c"), cst[0:12, :], [cst], [], "cso", is_out=True)
        if pre or STOP <= 4:
            return
        self.P.tag = "%s%d.p3" % (mode, tidx)
        self.bulk_banks = NBANKS
        vt = [cv("v%d" % s, 16 * K + s * 8 * K, [128, D], F32) for s in range(NS)]
        vn = [cv("vn%d" % s, 48 * K + s * 4 * K, [128, D], BF16) for s in range(NS)]
        ufm = cv("ufm", 64 * K, [128, 16, T], BF16)
        lgB = cv("lgB", 80 * K, [128, D], F32); lbB = cv("lbB", 88 * K, [128, D], F32)
        vst = cv("vst", 96 * K, [128, NS, 8], F32)
        self.dma("pool", lgB[:], self.w["sgu_ln_g"].partition_broadcast(128), [], [lgB], "lgB")
        self.dma("pool", lbB[:], self.w["sgu_ln_b"].partition_broadcast(128), [], [lbB], "lbB")
        self.ms("dve", vst[:], 0.0, [vst])
        for n in range(4):
            Wv = self.wload("w_in", 0, 16, C_V + n * 512, 512)
            for s in range(NS):
                pb = self.bulk()
                for kc in range(16):
                    self.mm(pb[0:L, :], h_fm[:, kc, s * L:(s + 1) * L], Wv[:, kc, :], kc == 0, kc == 15, [h_fm, Wv], [pb])
                self.act(vt[s][0:L, n * 512:(n + 1) * 512], pb[0:L, :], AF.Gelu_apprx_tanh, [pb], [vt[s], vst], accum_out=vst[0:L, s, n:n + 1])
        for s in range(NS):
            v, q = vt[s], vst
            self.act(vn[s][0:L, :], v[0:L, :], AF.Square, [v], [vn[s], q], accum_out=q[0:L, s, 4:5])
            self.red(q[0:L, s, 5:6], q[0:L, s, 0:4], [q], [q])
            self.ts("dve", q[0:L, s, 5:6], q[0:L, s, 5:6], 1.0 / D, None, ALU.mult, None, [q], [q])
            self.tt("dve", q[0:L, s, 6:7], q[0:L, s, 5:6], q[0:L, s, 5:6], ALU.mult, [q], [q])
            self.stt(q[0:L, s, 6:7], q[0:L, s, 4:5], 1.0 / D, q[0:L, s, 6:7], ALU.mult, ALU.subtract, [q], [q])
            self.rsqrt(q, q[0:L, s, 7:8], q, q[0:L, s, 6:7], 1.0, L, 1)
            self.ts("dve", v[0:L, :], v[0:L, :], q[0:L, s, 5:6], q[0:L, s, 7:8], ALU.subtract, ALU.mult, [v, q], [v])
            self.tt("dve", v[0:L, :], v[0:L, :], lgB[0:L, :], ALU.mult, [v, lgB], [v])
            self.tt("dve", v[0:L, :], v[0:L, :], lbB[0:L, :], ALU.add, [v, lbB], [v])
            self.cp("act", vn[s][0:L, :], v[0:L, :], [v], [vn[s]])
            if sample:
                self.dma("pool", self.v_s[s * L:(s + 1) * L, :], v[0:L, :], [v], [], ("vso", s), is_out=True)
        for n in range(4):
            Wu = self.wload("w_in", 0, 16, C_U + n * 512, 512)
            for j in range(4):
                pb = self.bulk()
                for kc in range(16):
                    self.mm(pb[:, 0:T], Wu[:, kc, j * 128:(j + 1) * 128], h_fm[:, kc, :], kc == 0, kc == 15, [Wu, h_fm], [pb])
                self.act(ufm[:, n * 4 + j, :], pb[:, 0:T], AF.Gelu_apprx_tanh, [pb], [ufm])
        for dk in range(16):
            g = dk // 2
            pb = self.bulk()
            for s in range(NS):
                self.mm(pb[:, s * L:(s + 1) * L], vn[s][0:L, dk * 128:(dk + 1) * 128], self.wmaskT[0:L, g, 0:L], True, False, [vn[s], self.wmaskT], [pb])
                self.mm(pb[:, s * L:(s + 1) * L], self.onesb[0:64, :], self.b2[0:64, g * 128:g * 128 + L], False, True, [self.onesb, self.b2], [pb])
            self.tt("dve", ufm[:, dk, :], pb[:, 0:T], ufm[:, dk, :], ALU.mult, [pb, ufm], [ufm])
        if STOP <= 5:
            return
        self.P.tag = "%s%d.p4" % (mode, tidx)
        m1 = cv("m1", 16 * K, [128, 16, T], BF16)
        mg = cv("mg", 32 * K, [128, 16, T], BF16)
        sg = [cv("sg%d" % i, 48 * K + i * 2 * K, [128, T], F32) for i in range(2)]
        sgb = cv("sgb", 52 * K, [128, 4, T], F32)
        for n in range(4):
            Wga = self.wload("w_in", 0, 16, C_GA + n * 512, 512)
            Wa = self.wload("w_a", 0, 16, n * 512, 512)
            for j in range(4):
                jj = n * 4 + j
                pa = self.bulk()
                for kc in range(16):
                    self.mm(pa[:, 0:T], Wga[:, kc, j * 128:(j + 1) * 128], h_fm[:, kc, :], kc == 0, kc == 15, [Wga, h_fm], [pa])
                sgt = sg[j % 2]
                self.act(sgt[:], pa[:, 0:T], AF.Sigmoid, [pa], [sgt])
                pb = self.bulk()
                for kc in range(16):
                    self.mm(pb[:, 0:T], Wa[:, kc, j * 128:(j + 1) * 128], ufm[:, kc, :], kc == 0, kc == 15, [Wa, ufm], [pb])
                self.tt("dve", m1[:, jj, :], sgt[:], pb[:, 0:T], ALU.mult, [sgt, pb], [m1])
        for n in range(4):
            Wgb = self.wload("w_in", 0, 16, C_GB + n * 512, 512)
            for j in range(4):
                pa = self.bulk()
                for kc in range(16):
                    self.mm(pa[:, 0:T], Wgb[:, kc, j * 128:(j + 1) * 128], h_fm[:, kc, :], kc == 0, kc == 15, [Wgb, h_fm], [pa])
                self.act(sgb[:, j, :], pa[:, 0:T], AF.Sigmoid, [pa], [sgb])
            Wb0 = self.wload("w_b", 0, 16, n * 512, 512)
            Wb1 = self.wload("w_b", 2048, 16, n * 512, 512)
            for j in range(4):
                jj = n * 4 + j
                pb = self.bulk()
                for kc in range(32):
                    Wb = Wb0 if kc < 16 else Wb1
                    self.mm(pb[:, 0:T], Wb[:, kc % 16, j * 128:(j + 1) * 128], ynfm[:, kc, :], kc == 0, kc == 31, [Wb, ynfm], [pb])
                sgt = sg[j % 2]
                self.tt("dve", sgt[:], sgb[:, j, :], pb[:, 0:T], ALU.mult, [sgb, pb], [sgt])
                self.tt("dve", mg[:, jj, :], sgt[:], m1[:, jj, :], ALU.add, [sgt, m1], [mg])
        if STOP <= 6:
            return
        self.P.tag = "%s%d.p5" % (mode, tidx)
        if sample:
            L, NS = 64, 1
        xt = [cv("xr%d" % s, 48 * K + s * 8 * K, [128, D], F32) for s in range(NS)]
        xb = [cv("xc%d" % i, 80 * K + i * 4 * K, [128, D], BF16) for i in range(2)]
        stat = [cv("stb%d" % i, 88 * K + i * 64, [128, 4], F32) for i in range(2)]
        for s in range(NS):
            self.dma("pool", xt[s][0:L, :], xsrc[s * L:(s + 1) * L, :], [], [xt[s]], ("xt", s))
        for n in range(4):
            Wo = self.wload("w_o", 0, 16, n * 512, 512)
            for s in range(NS):
                pb = self.bulk()
                for kc in range(16):
                    self.mm(pb[0:L, :], mg[:, kc, s * L:(s + 1) * L], Wo[:, kc, :], kc == 0, kc == 15, [mg, Wo], [pb])
                xs_ = xt[s][0:L, n * 512:(n + 1) * 512]
                self.tt("dve", xs_, xs_, pb[0:L, :], ALU.add, [xt[s], pb], [xt[s]])
        hf = cv("hf", 0, [128, 16, T], BF16)
        for s in range(NS):
            self.norm_to_fm(xt, s, L, self.gffnT, hf, xb[s % 2], stat[s % 2], (ps[2], ps[3]))
        if STOP <= 7:
            return
        self.P.tag = "%s%d.p6" % (mode, tidx)
        afm = cv("afm", 80 * K, [128, 44, T], BF16)
        sg = [cv("sf%d" % i, 16 * K + i * 2 * K, [128, T], F32) for i in range(2)]
        gF = cv("gF", 20 * K, [128, D], F32)
        junk = cv("junkf", 28 * K, [128, D], BF16)
        stf = cv("stf", 32 * K, [128, NS, 2], F32)
        self.dma("pool", gF[:], self.w["norm_final_g"].partition_broadcast(128), [], [gF], "gF")
        for f in range(11):
            Wg = self.wload("w_gate", 0, 16, f * 512, 512)
            Wu = self.wload("w_up", 0, 16, f * 512, 512)
            for j in range(4):
                pa = self.bulk()
                for kc in range(16):
                    self.mm(pa[:, 0:T], Wg[:, kc, j * 128:(j + 1) * 128], hf[:, kc, :], kc == 0, kc == 15, [Wg, hf], [pa])
                sgt = sg[j % 2]
                self.act(sgt[:], pa[:, 0:T], AF.Silu, [pa], [sgt])
                pb = self.bulk()
                for kc in range(16):
                    self.mm(pb[:, 0:T], Wu[:, kc, j * 128:(j + 1) * 128], hf[:, kc, :], kc == 0, kc == 15, [Wu, hf], [pb])
                self.tt("dve", afm[:, f * 4 + j, :], sgt[:], pb[:, 0:T], ALU.mult, [sgt, pb], [afm])
        pdn = (ps[0], ps[1], ps[2], ps[3])
        for n in range(4):
            for kg in range(4):
                Wd_ = self.wload("w_down", kg * 11 * 128, 11, n * 512, 512)
                for s in range(NS):
                    for kc in range(11):
                        self.mm(pdn[s][0:L, :], afm[:, kg * 11 + kc, s * L:(s + 1) * L], Wd_[:, kc, :], kg == 0 and kc == 0,
                                kg == 3 and kc == 10, [afm, Wd_], [pdn[s]])
            for s in range(NS):
                xs_ = xt[s][0:L, n * 512:(n + 1) * 512]
                self.tt("dve", xs_, xs_, pdn[s][0:L, :], ALU.add, [xt[s], pdn[s]], [xt[s]])
        self.ms("dve", stf[:], 0.0, [stf])
        for s in range(NS):
            x = xt[s]
            self.act(junk[0:L, :], x[0:L, :], AF.Square, [x], [junk, stf], accum_out=stf[0:L, s, 0:1])
            self.rsqrt(stf, stf[0:L, s, 1:2], stf, stf[0:L, s, 0:1], 1.0 / D, L, 1)
            self.stt(x[0:L, :], x[0:L, :], stf[0:L, s, 1:2], gF[0:L, :], ALU.mult, ALU.mult, [x, stf, gF], [x])
            self.dma("pool", ydst[s * L:(s + 1) * L, :], x[0:L, :], [x], [], ("yo", s), is_out=True)

    def store_prompt_state(self):
        K = 1024
        so = [self.carve("pso%d" % i, 100 * K + i * 2 * K, [128, 4, 128], F32) for i in range(2)]
        for g in range(NG):
            Sg = self.S[g]
            pU = self.ps[6 + g % 2]
            o = so[g % 2]
            for j in range(4):
                self.tr(pU[:, j * 128:(j + 1) * 128], Sg[:, j * 128:(j + 1) * 128], self.identf[:], [Sg, self.identf], [pU])
            self.cp("dve", o[:], pU[:, :].rearrange("p (a b) -> p a b", a=4), [pU], [o])
            self.dma("pool", self.ssm_p[8 * g:8 * g + 8].rearrange("h p n -> (h p) n").rearrange("(j q) n -> q j n", q=128),
                     o[:], [o], [], ("pso", g % 2), is_out=True)
        cst = self.carve("pcst", 104 * K, [16, CD], F32)
        for c0 in range(0, 48, 4):
            pb = self.bulk()
            for c in range(c0, c0 + 4):
                self.tr(pb[0:3, (c - c0) * 128:(c - c0 + 1) * 128], self.halo[:, c, :], self.identf[:], [self.halo, self.identf], [pb])
            self.cp("dve", cst[0:3, c0 * 128:(c0 + 4) * 128], pb[0:3, :], [pb], [cst])
        self.dma("pool", self.conv_p[:, :], cst[0:3, :], [cst], [], "pco", is_out=True)


_CACHE = {}


def _get_nc(key, **kw):
    if key not in _CACHE:
        _CACHE[key] = Builder(**kw).build()
    return _CACHE[key]


def kernel(x_prompt, x_sample, state_ssm, state_conv, norm_mix_g, w_in, sgu_ln_g, sgu_ln_b, sgu_w, sgu_b, w_a, conv_w, conv_b,
           dt_bias, a_log, d_skip, ssm_norm_g, w_b, w_o, norm_ffn_g, w_gate, w_up, w_down, norm_final_g):
    f = lambda a: np.ascontiguousarray(np.asarray(a, dtype=np.float32))
    nc = _get_nc("full")
    wts = dict(norm_mix_g=f(norm_mix_g[0]), w_in=f(w_in[0]), sgu_ln_g=f(sgu_ln_g[0]), sgu_ln_b=f(sgu_ln_b[0]), sgu_w=f(sgu_w[0]),
               sgu_b=f(sgu_b[0]), w_a=f(w_a[0]), conv_w=f(conv_w[0]), conv_b=f(conv_b[0]), dt_bias=f(dt_bias[0]), a_log=f(a_log[0]),
               d_skip=f(d_skip[0]), ssm_norm_g=f(ssm_norm_g[0]), w_b=f(w_b[0]), w_o=f(w_o[0]), norm_ffn_g=f(norm_ffn_g[0]),
               w_gate=f(w_gate[0]), w_up=f(w_up[0]), w_down=f(w_down[0]), norm_final_g=f(norm_final_g))
    x_prompt = np.asarray(x_prompt); x_sample = np.asarray(x_sample); state_ssm = np.asarray(state_ssm); state_conv = np.asarray(state_conv)
    in_maps = []
    for c in range(8):
        b, half = c // 2, c % 2
        m = dict(wts)
        m["xm"] = f(x_prompt[b, half * 4096:(half + 1) * 4096])
        m["xpre"] = f(x_prompt[b, 0:4096])
        m["flag"] = np.full((128, 1), float(half), np.float32)
        m["xs"] = f(x_sample[4 * c:4 * c + 4].reshape(64, D))
        m["sssm"] = f(state_ssm[0, 4 * c:4 * c + 4])
        m["sconv"] = f(state_conv[0, 4 * c:4 * c + 4])
        in_maps.append(m)
    res = run_bass_kernel_spmd(nc, in_maps, core_ids=list(range(8))).results
    y_prompt = np.empty((4, 8192, D), np.float32)
    for c in range(8):
        y_prompt[c // 2, (c % 2) * 4096:(c % 2 + 1) * 4096] = res[c]["yp"]
    y_sample = np.concatenate([res[c]["ys"].reshape(4, 16, D) for c in range(8)], 0)
    ssm_p = np.stack([res[2 * b + 1]["ssm_p"] for b in range(4)])[None]
    conv_p = np.stack([res[2 * b + 1]["conv_p"] for b in range(4)])[None]
    ssm_s = np.concatenate([res[c]["ssm_s"] for c in range(8)], 0)[None]
    conv_s = np.concatenate([res[c]["conv_s"] for c in range(8)], 0)[None]
    v_s = np.concatenate([res[c]["v_s"].reshape(4, 16, D) for c in range(8)], 0)[None]
    return (y_prompt, y_sample, ssm_p.astype(np.float32), conv_p.astype(np.float32), ssm_s.astype(np.float32),
            conv_s.astype(np.float32), v_s.astype(np.float32))
```

```python
import os
import numpy as np
from contextlib import ExitStack
import concourse.bass as bass
import concourse.mybir as mybir
from concourse.bass_utils import run_bass_kernel_spmd

F32 = mybir.dt.float32
BF16 = mybir.dt.bfloat16
U8 = mybir.dt.uint8
AF = mybir.ActivationFunctionType
ALU = mybir.AluOpType

D = 2048
DS = 4096
CD = 6144
NH = 64
NG = 8
DFF = 5632
DIN = 18496
C_U, C_V, C_Z, C_XS, C_B, C_C, C_DT, C_GA, C_GB = 0, 2048, 4096, 8192, 12288, 13312, 14336, 14400, 16448
EPS = 1e-6
STOP = float(os.environ.get('KSTOP', '99'))
SCHED = int(os.environ.get('KSCHED', '1'))
PE_MHZ = float(os.environ.get('KPEMHZ', '1930'))
WIN = [int(x) for x in os.environ.get('KWIN', '96,32,32,1').split(',')]
LATENCY = float(os.environ.get('KLAT', '0.15'))
P2BANKS = tuple(int(x) for x in os.environ.get('KP2BANKS', '0,1').split(','))
PREBANKS = tuple(int(x) for x in os.environ.get('KPREBANKS', '0,1,5,6').split(','))
XSLOTS = int(os.environ.get('KXSLOTS', '0'))
NBANKS = tuple(int(x) for x in os.environ.get('KBANKS', '0,1,5,6').split(','))
ENGS = ("pe", "act", "dve", "pool", "sp")


class Buf:
    _n = 0

    def __init__(self, ap, name="", rng=None):
        self.ap = ap
        self.name = name
        self.key = Buf._n
        Buf._n += 1
        self.rng = rng
        self.dead = False
        self.psum = False

    def __getitem__(self, idx):
        return self.ap[idx]


class Op:
    __slots__ = ("eng", "fn", "deps", "needs_inc", "inc_no", "dma_key", "dma_val", "odeps", "cost", "users", "nrem", "ready",
                 "fin", "pos", "tag", "kind")

    def __init__(self, eng, fn, dma_key):
        self.eng = eng
        self.fn = fn
        self.deps = []
        self.odeps = []
        self.cost = 0.3
        self.users = []
        self.nrem = 0
        self.ready = 0.0
        self.fin = -1.0
        self.pos = 0
        self.needs_inc = False
        self.inc_no = 0
        self.dma_key = dma_key
        self.dma_val = 0


class Prog:
    def __init__(self, nc):
        self.nc = nc
        self.ops = {e: [] for e in ENGS}
        self.lastw = {}
        self.readers = {}
        self.dma_last = {}
        self.dma_cnt = {}
        self.alias = {}
        self.tag = ""

    def add(self, eng, fn, reads=(), writes=(), dma_key=None, extra=(), cost=0.3):
        op = Op(eng, fn, dma_key)
        op.cost = cost
        op.tag = self.tag
        op.kind = "%s:%.2f" % (getattr(fn, "__qualname__", "?").split(".")[1] if "." in getattr(fn, "__qualname__", "") else "?", cost)
        deps = {}
        for b in reads:
            assert not b.dead, b.name
            w = self.lastw.get(b.key)
            if w is not None:
                deps[id(w)] = (w, "raw")
            if b.psum:
                for r in self.readers.get(b.key, ()):
                    if r.eng != eng and id(r) not in deps:
                        deps[id(r)] = (r, "raw")
        for b in writes:
            assert not b.dead, b.name
            w = self.lastw.get(b.key)
            if w is not None:
                deps[id(w)] = (w, "waw")
            for r in self.readers.get(b.key, ()):
                if id(r) not in deps:
                    deps[id(r)] = (r, "war")
            for r in self.alias.pop(b.key, ()):
                deps[id(r)] = (r, "waw")
        for x in extra:
            deps[id(x)] = (x, "raw")
        if dma_key is not None:
            p = self.dma_last.get(dma_key)
            if p is not None:
                deps[id(p)] = (p, "dma")
            self.dma_last[dma_key] = op
            self.dma_cnt[dma_key] = self.dma_cnt.get(dma_key, 0) + 1
            op.dma_val = 16 * self.dma_cnt[dma_key]
        for d, kind in deps.values():
            if d is op:
                continue
            if d.dma_key is None and d.eng == eng and dma_key is None:
                if eng == "pe":
                    op.odeps.append(d)
                    continue
            op.deps.append(d)
            if d.dma_key is None:
                d.needs_inc = True
        for b in reads:
            self.readers.setdefault(b.key, []).append(op)
        for b in writes:
            self.lastw[b.key] = op
            self.readers[b.key] = []
        self.ops[eng].append(op)
        return op

    def retarget(self, old_bufs, new_buf):
        pend = []
        for b in old_bufs:
            w = self.lastw.get(b.key)
            if w is not None:
                pend.append(w)
            pend.extend(self.readers.get(b.key, ()))
            b.dead = True
        self.alias.setdefault(new_buf.key, []).extend(pend)

    def schedule(self, window):
        LAT = LATENCY
        allops = []
        for e in ENGS:
            for i, op in enumerate(self.ops[e]):
                op.pos = i
                allops.append(op)
        for op in allops:
            ds = {id(d): d for d in op.deps}
            for d in op.odeps:
                ds[id(d)] = d
            op.nrem = len(ds)
            for d in ds.values():
                d.users.append(op)
        queues = {e: list(self.ops[e]) for e in ENGS}
        heads = {e: 0 for e in ENGS}
        done = {e: [False] * len(queues[e]) for e in ENGS}
        free = {e: 0.0 for e in ENGS}
        order = {e: [] for e in ENGS}
        ntot = len(allops)
        nsch = 0
        while nsch < ntot:
            best = None
            for e in ENGS:
                q = queues[e]
                h = heads[e]
                dn = done[e]
                n = len(q)
                if h >= n:
                    continue
                w = window.get(e, 1)
                fe = free[e]
                cnt = 0
                i = h
                while i < n and cnt < w:
                    if not dn[i]:
                        cnt += 1
                        op = q[i]
                        if op.nrem == 0:
                            st = op.ready if op.ready > fe else fe
                            if best is None or st < best[0] - 1e-9:
                                best = (st, e, i, op)
                            if st <= fe:
                                break
                    i += 1
            st, e, i, op = best
            op.fin = st + op.cost
            free[e] = op.fin if op.dma_key is None else st + 0.05
            done[e][i] = True
            order[e].append(op)
            while heads[e] < len(queues[e]) and done[e][heads[e]]:
                heads[e] += 1
            fin = op.fin + LAT
            for u in op.users:
                u.nrem -= 1
                if fin > u.ready:
                    u.ready = fin
            nsch += 1
        self.ops = order
        return max(free.values())

    def emit(self, stack, final_wait_keys=()):
        nc = self.nc
        esem = {e: stack.enter_context(nc.semaphore("s_" + e)) for e in ENGS}
        dsem = {}
        for i, k in enumerate(self.dma_cnt):
            dsem[k] = stack.enter_context(nc.semaphore("d%d" % i))
        for e in ENGS:
            n = 0
            for op in self.ops[e]:
                if op.dma_key is None and op.needs_inc:
                    n += 1
                    op.inc_no = n
        block = stack.enter_context(nc.Block())

        def run(ename, eh):
            seen = {}
            for op in self.ops[ename]:
                for d in op.deps:
                    if d.dma_key is not None:
                        s, v = dsem[d.dma_key], d.dma_val
                    else:
                        s, v = esem[d.eng], d.inc_no
                    sid = id(s)
                    if seen.get(sid, 0) >= v:
                        continue
                    seen[sid] = v
                    eh.wait_ge(s, v)
                ins = op.fn(eh)
                if op.dma_key is not None:
                    ins.then_inc(dsem[op.dma_key], 16)
                elif op.needs_inc:
                    ins.then_inc(esem[ename], 1)
            if ename == "sp":
                for k in final_wait_keys:
                    eh.wait_ge(dsem[k], 16 * self.dma_cnt[k])

        @block.tensor
        def _(e):
            run("pe", e)

        @block.scalar
        def _(e):
            run("act", e)

        @block.vector
        def _(e):
            run("dve", e)

        @block.gpsimd
        def _(e):
            run("pool", e)

        @block.sync
        def _(e):
            run("sp", e)


class Builder:
    def __init__(self, n_main=8, n_pre=8, do_sample=True, do_convert=True):
        self.n_main, self.n_pre, self.do_sample = n_main, n_pre, do_sample
        self.do_convert = do_convert
        self.nc = nc = bass.Bass("TRN2", target_bir_lowering=False)
        self.st = ExitStack()
        self.P = Prog(nc)
        self.out_keys = []
        self.live = []
        self.zomb = []
        di = lambda n, s: nc.dram_tensor(n, s, F32, kind="ExternalInput").ap()
        do = lambda n, s: nc.dram_tensor(n, s, F32, kind="ExternalOutput").ap()
        nm, npre = max(n_main, 1) * 512, max(n_pre, 1) * 512
        self.xm = di("xm", [nm, D]); self.xpre = di("xpre", [npre, D]); self.flag_in = di("flag", [128, 1])
        self.xs_in = di("xs", [64, D]); self.sssm = di("sssm", [4, NH, 64, 128]); self.sconv = di("sconv", [4, 3, CD])
        self.w = {}
        for n, s in (("norm_mix_g", [D]), ("w_in", [D, DIN]), ("sgu_ln_g", [D]), ("sgu_ln_b", [D]), ("sgu_w", [NG, 128, 128]),
                     ("sgu_b", [NG, 128]), ("w_a", [D, D]), ("conv_w", [4, CD]), ("conv_b", [CD]), ("dt_bias", [NH]),
                     ("a_log", [NH]), ("d_skip", [NH]), ("ssm_norm_g", [DS]), ("w_b", [DS, D]), ("w_o", [D, D]),
                     ("norm_ffn_g", [D]), ("w_gate", [D, DFF]), ("w_up", [D, DFF]), ("w_down", [DFF, D]), ("norm_final_g", [D])):
            self.w[n] = di(n, s)
        self.yp = do("yp", [nm, D]); self.ys = do("ys", [64, D]); self.ssm_p = do("ssm_p", [NH, 64, 128])
        self.conv_p = do("conv_p", [3, CD]); self.ssm_s = do("ssm_s", [4, NH, 64, 128]); self.conv_s = do("conv_s", [4, 3, CD])
        self.v_s = do("v_s", [64, D])
        self.wbf = {}
        for n, s in (("w_in", [D, DIN]), ("w_a", [D, D]), ("w_b", [DS, D]), ("w_o", [D, D]), ("w_gate", [D, DFF]),
                     ("w_up", [D, DFF]), ("w_down", [DFF, D])):
            self.wbf[n] = nc.dram_tensor(n + "_bf", s, BF16, kind="Internal").ap()
        self.cdone = {}
        self.cv_pending = []
        self.cv_bg = False
        self.cv_rate = 0
        self.cv_k = {0: 0, 1: 0}
        self.wcount = 0

    def sb(self, name, shape, dt=F32):
        return Buf(self.st.enter_context(self.nc.sbuf_tensor(name, shape, dt)), name)

    def carve(self, name, off, shape, dt):
        esz = 2 if dt == BF16 else 4
        n = 1
        for d in shape[1:]:
            n *= d
        nb = n * esz
        assert off % 4 == 0 and off + nb <= self.ARENA, (name, off, nb)
        ap = self.arena[:, off:off + nb].bitcast(dt)
        if len(shape) == 3:
            ap = ap.rearrange("p (a b) -> p a b", a=shape[1])
        elif len(shape) == 4:
            ap = ap.rearrange("p (a b c) -> p a b c", a=shape[1], b=shape[2])
        b = Buf(ap, name, (off, off + nb))
        lo, hi = off, off + nb
        old = [o for o in self.live if o.rng[0] < hi and lo < o.rng[1]]
        pend = []
        for o in old:
            w = self.P.lastw.get(o.key)
            p = ([w] if w is not None else []) + list(self.P.readers.get(o.key, ())) + list(self.P.alias.get(o.key, ()))
            o.dead = True
            self.zomb.append((o.rng, p))
        self.live = [o for o in self.live if not o.dead]
        for rng, p in self.zomb:
            if rng[0] < hi and lo < rng[1]:
                pend.extend(p)
        if pend:
            self.P.alias.setdefault(b.key, []).extend(pend)
        self.live.append(b)
        return b

    @staticmethod
    def nfree(ap):
        n = 1
        for d in ap.shape[1:]:
            n *= d
        return n

    def ecost(self, eng, out):
        n = self.nfree(out)
        if eng == "act":
            return 0.26 + n * 0.00085
        if eng == "dve":
            return 0.12 + n * 0.00105
        return 0.35 + n * 0.0021

    def mm(self, out, lhsT, rhs, start, stop, reads, writes):
        n = self.nfree(rhs)
        c = max(n, 64) / PE_MHZ * (4.0 if rhs.dtype == F32 else 1.0) + 0.012
        return self.P.add("pe", lambda e: e.matmul(out, lhsT=lhsT, rhs=rhs, start=start, stop=stop), reads, writes, cost=c)

    def tr(self, out, in_, ident, reads, writes):
        c = 0.09 * (4.0 if in_.dtype == F32 else 1.0)
        return self.P.add("pe", lambda e: e.transpose(out=out, in_=in_, identity=ident), reads, writes, cost=c)

    def act(self, out, in_, func, reads, writes, **kw):
        return self.P.add("act", lambda e: e.activation(out=out, in_=in_, func=func, **kw), reads, writes, cost=self.ecost("act", out))

    def tt(self, eng, out, in0, in1, op, reads, writes):
        return self.P.add(eng, lambda e: e.tensor_tensor(out=out, in0=in0, in1=in1, op=op), reads, writes, cost=self.ecost(eng, out))

    def ts(self, eng, out, in0, s1, s2, op0, op1, reads, writes):
        c = self.ecost(eng, out)
        if s2 is None:
            return self.P.add(eng, lambda e: e.tensor_scalar(out=out, in0=in0, scalar1=s1, scalar2=None, op0=op0), reads, writes, cost=c)
        return self.P.add(eng, lambda e: e.tensor_scalar(out=out, in0=in0, scalar1=s1, scalar2=s2, op0=op0, op1=op1), reads, writes, cost=c)

    def stt(self, out, in0, scalar, in1, op0, op1, reads, writes):
        return self.P.add("dve", lambda e: e.scalar_tensor_tensor(out=out, in0=in0, scalar=scalar, in1=in1, op0=op0, op1=op1), reads, writes,
                          cost=self.ecost("dve", out))

    def cp(self, eng, out, in_, reads, writes):
        if eng == "act":
            return self.act(out, in_, AF.Copy, reads, writes)
        return self.P.add(eng, lambda e: e.tensor_copy(out=out, in_=in_), reads, writes, cost=self.ecost(eng, out))

    def red(self, out, in_, reads, writes):
        return self.P.add("dve", lambda e: e.tensor_reduce(out=out, in_=in_, axis=mybir.AxisListType.X, op=ALU.add), reads, writes)

    def ms(self, eng, ap, val, writes):
        return self.P.add(eng, lambda e: e.memset(ap, val), (), writes)

    def dma(self, q, out, in_, reads, writes, key, extra=(), is_out=False):
        if is_out:
            self.out_keys.append(key)
        c = 2.0 + self.nfree(out) * out.shape[0] * 4 / 200e3
        return self.P.add(q, lambda e: e.dma_start(out=out, in_=in_), reads, writes, dma_key=key, extra=extra, cost=c)

    def rsqrt(self, out_buf, out, in_buf, in_, scale, npart, ncols):
        self.ts("dve", out, in_, scale, EPS, ALU.mult, ALU.add, [in_buf], [out_buf])
        self.tt("pool", out, out, self.mhalf[0:npart, 0:ncols], ALU.pow, [out_buf, self.mhalf], [out_buf])

    def wload(self, name, row0, kc, col0, width):
        i = self.wcount % 3
        self.wcount += 1
        slot = self.wslot[i]
        src = self.wbf[name][row0:row0 + kc * 128, col0:col0 + width].rearrange("(kc p) n -> p kc n", p=128)
        key = name
        if name == "w_in":
            key = "w_in_pre" if C_XS <= col0 < C_GA else "w_in_rest"
        assert not [j for j in self.cv_pending if j[1] == key], key
        extra = list(self.cdone.get(key, {}).values())
        self.P.add("sp", lambda e: e.dma_start(out=slot[:, 0:kc, 0:width], in_=src), (), [slot], dma_key=("w", i), extra=extra,
                   cost=2.0 + kc * width * 256 / 250e3)
        if self.cv_bg:
            for _ in range(self.cv_rate):
                if self.cv_pending:
                    self.cv_job(self.cv_pending.pop(0), 1)
        return slot

    def cv_job(self, job, mode):
        name, key, rb, c0, cw = job
        slots = self.cv_slots[mode]
        k = self.cv_k[mode]
        self.cv_k[mode] += 1
        si = k % len(slots)
        f, b = slots[si]
        src, dst = self.w[name], self.wbf[name]
        self.dma("sp", f[:, 0:cw], src[rb * 128:(rb + 1) * 128, c0:c0 + cw], [], [f], ("cvl", mode, si))
        if mode == 0:
            self.cp("dve" if k % 2 == 0 else "act", b[:, 0:cw], f[:, 0:cw], [f], [b])
            op = self.dma("pool", dst[rb * 128:(rb + 1) * 128, c0:c0 + cw], b[:, 0:cw], [b], [], ("cvs", mode, si))
        else:
            self.cp("act", b[:, 0:cw], f[:, 0:cw], [f], [b])
            op = self.dma("act", dst[rb * 128:(rb + 1) * 128, c0:c0 + cw], b[:, 0:cw], [b], [], ("cvs", mode, si))
        self.cdone.setdefault(key, {})[(mode, si)] = op

    def featmajor(self, rows, dst_buf, dst_ap, stage_buf, n, tag):
        R = len(rows)
        for r, row in enumerate(rows):
            self.dma("pool", stage_buf[r:r + 1, 0:n * 128], row.rearrange("(o n) -> o n", o=1), [], [stage_buf], ("fm", tag, r))
        per = 512 // R
        j0 = 0
        while j0 < n:
            jn = min(per, n - j0)
            pb = self.psb[0]
            for j in range(jn):
                self.tr(pb[:, j * R:(j + 1) * R], stage_buf[0:R, (j0 + j) * 128:(j0 + j + 1) * 128], self.identf[0:R, 0:R],
                        [stage_buf, self.identf], [pb])
            self.cp("dve", dst_ap[:, j0:j0 + jn, :], pb[:, 0:jn * R].rearrange("p (a b) -> p a b", a=jn), [pb], [dst_buf])
            j0 += jn

    def build(self):
        nc, P = self.nc, self.P
        self.ARENA = 130 * 1024
        self.arena = self.st.enter_context(nc.sbuf_tensor("arena", [128, self.ARENA], U8))
        sb = self.sb
        self.wslot = [sb("wslot%d" % i, [128, 16, 512], BF16) for i in range(3)]
        self.S = [sb("S%d" % g, [128, 512], F32) for g in range(NG)]
        self.identf = sb("identf", [128, 128]); self.ident = sb("ident", [128, 128], BF16)
        self.trif = sb("trif", [128, 128]); self.trib = sb("trib", [128, 128], BF16)
        self.negb = sb("negb", [128, 128], BF16); self.onesf = sb("onesf", [128, 128]); self.onesb = sb("onesb", [128, 128], BF16)
        self.wmaskT = sb("wmaskT", [128, NG, 128], BF16); self.b2 = sb("b2", [64, NG * 128], BF16)
        self.gmixT = sb("gmixT", [128, 16, 1]); self.gffnT = sb("gffnT", [128, 16, 1]); self.gssmT = sb("gssmT", [128, 32, 1])
        self.convw = sb("convw", [128, 48, 5])
        self.dtb = sb("dtb", [128, NH]); self.abc = sb("abc", [128, NH]); self.dsk = sb("dsk", [128, NH])
        self.flag = sb("flagt", [128, 1]); self.mhalf = sb("mhalf", [128, 8]); self.halo = sb("halo", [128, 48, 3])
        self.ps = [Buf(self.st.enter_context(nc.psum_tensor("ps%d" % i, [128, 512], F32)), "ps%d" % i) for i in range(8)]
        for b in self.ps:
            b.psum = True
        self.psb = self.ps
        self.pSm_sub = [Buf(self.ps[4].ap, "pSm_%d" % i) for i in range(3)]
        self.bulk_i = 0
        self.bulk_banks = (0, 1)
        self.consts()
        if self.do_convert:
            self.convert_weights()
        for g in range(NG):
            self.ms("dve", self.S[g][:], 0.0, [self.S[g]])
        self.ms("dve", self.halo[:], 0.0, [self.halo])
        self.cv_bg = True
        for t in range(self.n_pre):
            self.tile("pre", self.xpre[t * 512:(t + 1) * 512, :], None, t)
        self.cv_flush()
        if self.n_pre > 0:
            for g in range(NG):
                self.ts("dve", self.S[g][:], self.S[g][:], self.flag[:, 0:1], None, ALU.mult, None, [self.S[g], self.flag], [self.S[g]])
            self.ts("dve", self.halo[:], self.halo[:], self.flag[:, 0:1], None, ALU.mult, None, [self.halo, self.flag], [self.halo])
        for t in range(self.n_main):
            self.tile("main", self.xm[t * 512:(t + 1) * 512, :], self.yp[t * 512:(t + 1) * 512, :], t)
        if self.n_main > 0:
            self.store_prompt_state()
        if self.do_sample:
            self.tile("sample", self.xs_in, self.ys, 0)
        if SCHED:
            self.est = P.schedule({"pe": WIN[0], "act": WIN[1], "dve": WIN[2], "pool": WIN[3], "sp": 1})
        P.emit(self.st, final_wait_keys=list(dict.fromkeys(self.out_keys)))
        self.st.close()
        return nc

    def consts(self):
        P = self.P
        self.ms("pool", self.identf[:], 1.0, [self.identf])
        P.add("pool", lambda e: e.affine_select(out=self.identf[:], in_=self.identf[:], pattern=[[-1, 128]], compare_op=ALU.is_equal,
                                                fill=0.0, base=0, channel_multiplier=1), [self.identf], [self.identf])
        self.ms("pool", self.trif[:], 1.0, [self.trif])
        P.add("pool", lambda e: e.affine_select(out=self.trif[:], in_=self.trif[:], pattern=[[1, 128]], compare_op=ALU.is_ge,
                                                fill=0.0, base=0, channel_multiplier=-1), [self.trif], [self.trif])
        negf = self.carve("negf", 0, [128, 128], F32)
        self.ms("pool", negf[:], -30000.0, [negf])
        P.add("pool", lambda e: e.affine_select(out=negf[:], in_=negf[:], pattern=[[-1, 128]], compare_op=ALU.is_gt,
                                                fill=0.0, base=0, channel_multiplier=1), [negf], [negf])
        self.ms("pool", self.onesf[:], 1.0, [self.onesf])
        self.ms("pool", self.mhalf[:], -0.5, [self.mhalf])
        self.cp("dve", self.ident[:], self.identf[:], [self.identf], [self.ident])
        self.cp("dve", self.trib[:], self.trif[:], [self.trif], [self.trib])
        self.cp("dve", self.negb[:], negf[:], [negf], [self.negb])
        self.cp("dve", self.onesb[:], self.onesf[:], [self.onesf], [self.onesb])
        w = self.w
        stage = self.carve("cstage", 1024, [128, CD], F32)
        self.featmajor([w["norm_mix_g"]], self.gmixT, self.gmixT[:], stage, 16, "gm")
        self.featmajor([w["norm_ffn_g"]], self.gffnT, self.gffnT[:], stage, 16, "gf")
        self.featmajor([w["ssm_norm_g"]], self.gssmT, self.gssmT[:], stage, 32, "gs")
        self.featmajor([w["conv_w"][k, :] for k in range(4)] + [w["conv_b"]], self.convw, self.convw[:], stage, 48, "cw")
        self.dma("pool", self.dtb[:], w["dt_bias"].partition_broadcast(128), [], [self.dtb], "dtb")
        self.dma("pool", self.abc[:], w["a_log"].partition_broadcast(128), [], [self.abc], "abc")
        self.dma("pool", self.dsk[:], w["d_skip"].partition_broadcast(128), [], [self.dsk], "dsk")
        self.dma("pool", self.flag[:], self.flag_in[:, :], [], [self.flag], "flag")
        self.act(self.abc[:], self.abc[:], AF.Exp, [self.abc], [self.abc])
        self.ts("dve", self.abc[:], self.abc[:], -1.0, None, ALU.mult, None, [self.abc], [self.abc])
        wst = self.carve("wst", 32 * 1024, [128, NG, 128], F32)
        self.dma("pool", wst[:], w["sgu_w"].rearrange("g t s -> t g s"), [], [wst], "wst")
        for g in range(NG):
            pb = self.ps[g % 2]
            self.tr(pb[:, 0:128], wst[:, g, :], self.identf[:], [wst, self.identf], [pb])
            self.tt("dve", self.wmaskT[:, g, :], pb[:, 0:128], self.trif[:], ALU.mult, [pb, self.trif], [self.wmaskT])
        bst = self.carve("bst", 40 * 1024, [64, NG * 128], F32)
        bhi = self.carve("bhi", 48 * 1024, [64, NG * 128], BF16)
        self.ms("dve", self.b2[:], 0.0, [self.b2])
        brow = w["sgu_b"].rearrange("(o g) t -> o (g t)", o=1)
        self.dma("pool", bst[0:1, :], brow, [], [bst], "bst0")
        self.dma("pool", bst[32:33, :], brow, [], [bst], "bst1")
        self.cp("dve", self.b2[0:1, :], bst[0:1, :], [bst], [self.b2])
        self.cp("dve", bhi[32:33, :], bst[32:33, :], [bst], [bhi])
        self.tt("dve", self.b2[32:33, :], bst[32:33, :], bhi[32:33, :], ALU.subtract, [bst, bhi], [self.b2])

    def convert_weights(self):
        jobs0, jobs1 = [], []

        def addm(lst, name, key, c_lo, c_hi, cwmax):
            for rb in range(self.w[name].shape[0] // 128):
                for c0 in range(c_lo, c_hi, cwmax):
                    lst.append((name, key, rb, c0, min(cwmax, c_hi - c0)))

        addm(jobs0, "w_in", "w_in_pre", C_XS, C_GA, 3104)
        CWS = 1792
        addm(jobs1, "w_in", "w_in_rest", 0, C_XS, CWS)
        addm(jobs1, "w_in", "w_in_rest", C_GA, DIN, CWS)
        for n in ("w_a", "w_b", "w_o", "w_gate", "w_up", "w_down"):
            addm(jobs1, n, n, 0, self.w[n].shape[1], CWS)
        big = []
        for i in range(3):
            f = self.carve("cvf%d" % i, i * 24576, [128, 4096], F32)
            b = self.carve("cvb%d" % i, i * 24576 + 16384, [128, 4096], BF16)
            big.append((f, b))
        self.cv_slots = {0: big}
        for j in jobs0:
            self.cv_job(j, 0)
        if self.n_pre == 0:
            for j in jobs1:
                self.cv_job(j, 0)
        else:
            small = []
            for i, off in enumerate((97 * 1024, 97 * 1024 + 10752, 97 * 1024 + 21504, 70 * 1024, 86 * 1024)):
                f = self.carve("cwf%d" % i, off, [128, CWS], F32)
                b = self.carve("cwb%d" % i, off + CWS * 4, [128, CWS], BF16)
                small.append((f, b))
            self.cv_slots[1] = small
            self.cv_pending = jobs1
            self.cv_rate = -(-len(jobs1) // (13 * self.n_pre - 2))

    def cv_flush(self):
        while self.cv_pending:
            self.cv_job(self.cv_pending.pop(0), 1)
        self.cv_bg = False

    def norm_to_fm(self, xt, s, L, gT, h_fm, xb, stat, pbs):
        x = xt[s]
        self.ms("dve", stat[0:L, 0:1], 0.0, [stat])
        self.act(xb[0:L, :], x[0:L, :], AF.Square, [x], [xb, stat], accum_out=stat[0:L, 0:1])
        self.rsqrt(stat, stat[0:L, 1:2], stat, stat[0:L, 0:1], 1.0 / D, L, 1)
        self.ts("dve", xb[0:L, :], x[0:L, :], stat[0:L, 1:2], None, ALU.mult, None, [x, stat], [xb])
        for half in range(2):
            pb = pbs[half]
            pv = pb[:].bitcast(BF16)
            for j in range(8):
                kc = half * 8 + j
                self.tr(pv[:, j * L:(j + 1) * L], xb[0:L, kc * 128:(kc + 1) * 128], self.ident[0:L, 0:L], [xb, self.ident], [pb])
            self.tt("dve", h_fm[:, half * 8:half * 8 + 8, s * L:(s + 1) * L], pv[:, 0:8 * L].rearrange("p (a b) -> p a b", a=8),
                    gT[:, half * 8:half * 8 + 8, :].to_broadcast([128, 8, L]), ALU.mult, [pb, gT], [h_fm])

    def bulk(self):
        banks = self.bulk_banks
        self.bulk_i = (self.bulk_i + 1) % len(banks)
        return self.ps[banks[self.bulk_i]]

    def tile(self, mode, xsrc, ydst, tidx):
        self.P.tag = "%s%d" % (mode, tidx)
        sample = mode == "sample"
        pre = mode == "pre"
        L = 16 if sample else 128
        NS = 4
        T = L * NS
        NSEG, LSEG = (4, 16) if sample else (1, 512)
        K = 1024
        cv = self.carve
        ps = self.ps
        ident, identf = self.ident, self.identf
        L_ssd, NS_ssd = L, NS
        if sample:
            L, NS = 64, 1
        h_fm = cv("h_fm", 0, [128, 16, T], BF16)
        xt = [cv("xt%d" % s, 16 * K + s * 8 * K, [128, D], F32) for s in range(NS)]
        xb = [cv("xb%d" % i, 48 * K + i * 4 * K, [128, D], BF16) for i in range(2)]
        stat = [cv("stat%d" % i, 56 * K + i * 64, [128, 4], F32) for i in range(2)]
        for s in range(NS):
            self.dma("pool", xt[s][0:L, :], xsrc[s * L:(s + 1) * L, :], [], [xt[s]], ("xt", s))
        for s in range(NS):
            self.norm_to_fm(xt, s, L, self.gmixT, h_fm, xb[s % 2], stat[s % 2], (ps[0], ps[1]))
        L, NS = L_ssd, NS_ssd
        if STOP <= 1:
            return
        self.P.tag = "%s%d.p2" % (mode, tidx)
        self.bulk_banks = PREBANKS if pre else P2BANKS
        o = 16 * K
        dtr = cv("dtr", o, [128, NS, NH], F32); dt = cv("dt", o + K, [128, NS, NH], F32); da = cv("da", o + 2 * K, [128, NS, NH], F32)
        negcum = cv("negcum", o + 3 * K, [128, NS, NH], F32); ecum = cv("ecum", o + 4 * K, [128, NS, NH], F32)
        edec = cv("edec", o + 5 * K, [128, NS, NH], F32); dtt = cv("dtt", o + 6 * K, [128, NS, NH], F32)
        dahi = cv("dahi", o + 7 * K, [128, NS, NH], BF16); dalo = cv("dalo", o + 7 * K + 512, [128, NS, NH], BF16)
        BC = cv("BC", 24 * K, [128, 16, T], BF16)
        xraw = [cv("xraw%d" % i, 40 * K + i * 2304, [128, NSEG, 3 + LSEG], F32) for i in range(2)]
        cacc = [cv("cacc%d" % i, 45 * K + i * 2 * K, [128, NSEG, LSEG], F32) for i in range(2)]
        if pre and XSLOTS:
            xraw += [cv("xraw%d" % (2 + i), 57 * K + i * 2304, [128, NSEG, 3 + LSEG], F32) for i in range(2)]
            cacc += [cv("cacc%d" % (2 + i), 61 * K + 512 + i * 2 * K, [128, NSEG, LSEG], F32) for i in range(1)]
        xsfm = [cv("xsfm%d" % i, 49 * K + i * 4 * K, [128, 4, T], BF16) for i in range(2)]
        zs = [None, None] if pre else [cv("zs%d" % i, 57 * K + i * 4 * K, [128, NS, 512], BF16) for i in range(2)]
        if sample:
            sconvT = cv("sconvT", 28 * K, [128, 48, 12], F32)
            convo = cv("convo", 31 * K, [128, 48, 12], F32)
            sst = [cv("sst%d" % i, 109 * K + i * 2 * K, [128, 4, 128], F32) for i in range(4)]
            sso = [cv("sso%d" % i, 117 * K + i * 2 * K, [128, 4, 128], F32) for i in range(4)]
            Ssmp = [cv("Ssmp%d" % i, 101 * K + i * 2 * K, [128, 512], F32) for i in range(4)]
        if not pre:
            ynfm = cv("ynfm", 97 * K, [128, 32, T], BF16)
        Wd = self.wload("w_in", 0, 16, C_DT, 64)
        pdt, pcum, pcl = ps[2], ps[3], ps[7]
        for s in range(NS):
            for kc in range(16):
                self.mm(pdt[0:L, s * NH:(s + 1) * NH], h_fm[:, kc, s * L:(s + 1) * L], Wd[:, kc, 0:NH], kc == 0, kc == 15, [h_fm, Wd], [pdt])
        if STOP <= 1.2:
            return
        self.tt("dve", dtr[0:L], pdt[0:L, 0:NS * NH].rearrange("p (a b) -> p a b", a=NS),
                self.dtb[0:L, :].unsqueeze(1).to_broadcast([L, NS, NH]), ALU.add, [pdt, self.dtb], [dtr])
        self.act(dtr[0:L], dtr[0:L], AF.Exp, [dtr], [dtr])
        self.act(dt[0:L], dtr[0:L], AF.Ln, [dtr], [dt], bias=1.0, scale=1.0)
        self.tt("dve", da[0:L], dt[0:L], self.abc[0:L, :].unsqueeze(1).to_broadcast([L, NS, NH]), ALU.mult, [dt, self.abc], [da])
        self.act(dtr[0:L], dt[0:L], AF.Ln, [dt], [dtr])
        self.cp("dve", dahi[0:L], da[0:L], [da], [dahi])
        self.tt("dve", dalo[0:L], da[0:L], dahi[0:L], ALU.subtract, [da, dahi], [dalo])
        if STOP <= 1.4:
            return
        for s in range(NS):
            self.mm(pcum[0:L, s * NH:(s + 1) * NH], self.trif[0:L, 0:L], da[0:L, s, :], True, True, [self.trif, da], [pcum])
            self.mm(pcl[:, s * NH:(s + 1) * NH], self.onesf[0:L, :], da[0:L, s, :], True, True, [self.onesf, da], [pcl])
        if STOP <= 1.6:
            return
        c3 = lambda pb, n: pb[0:n, 0:NS * NH].rearrange("p (a b) -> p a b", a=NS)
        self.ts("dve", negcum[0:L], c3(pcum, L), -1.0, None, ALU.mult, None, [pcum], [negcum])
        self.tt("dve", dtr[0:L], dtr[0:L], negcum[0:L], ALU.add, [dtr, negcum], [dtr])
        self.act(ecum[0:L], c3(pcum, L), AF.Exp, [pcum], [ecum])
        self.act(edec[:], c3(pcl, 128), AF.Exp, [pcl], [edec])
        self.tt("dve", dtt[0:L], c3(pcl, L), negcum[0:L], ALU.add, [pcl, negcum], [dtt])
        self.act(dtt[0:L], dtt[0:L], AF.Exp, [dtt], [dtt])
        self.tt("dve", dtt[0:L], dtt[0:L], dt[0:L], ALU.mult, [dtt, dt], [dtt])
        if STOP <= 2:
            return
        if sample:
            stg = cv("cstg", 65 * K, [16, CD], F32)
            for q in range(4):
                rows = [self.sconv[q, k, :] for k in range(3)]
                for r, row in enumerate(rows):
                    self.dma("pool", stg[r:r + 1, :], row.rearrange("(o n) -> o n", o=1), [], [stg], ("fm", "sc", r))
                j0 = 0
                while j0 < 48:
                    jn = min(128, 48 - j0)
                    pb = ps[0]
                    for j in range(jn):
                        self.tr(pb[:, j * 3:(j + 1) * 3], stg[0:3, (j0 + j) * 128:(j0 + j + 1) * 128], identf[0:3, 0:3], [stg, identf], [pb])
                    self.cp("dve", sconvT[:, j0:j0 + jn, q * 3:(q + 1) * 3], pb[:, 0:jn * 3].rearrange("p (a b) -> p a b", a=jn), [pb], [sconvT])
                    j0 += jn
        self.cv_i = 0

        def conv_chunk(W, j, c, out_ap, out_buf):
            self.cv_i += 1
            pb = self.bulk()
            for kc in range(16):
                self.mm(pb[:, 0:T], W[:, kc, j * 128:(j + 1) * 128], h_fm[:, kc, :], kc == 0, kc == 15, [W, h_fm], [pb])
            xr, ca = xraw[self.cv_i % len(xraw)], cacc[self.cv_i % len(cacc)]
            if sample:
                self.cp("pool", xr[:, :, 0:3], sconvT[:, c, :].rearrange("p (a b) -> p a b", a=4), [sconvT], [xr])
            else:
                self.cp("pool", xr[:, 0, 0:3], self.halo[:, c, :], [self.halo], [xr])
            self.cp("act", xr[:, :, 3:3 + LSEG], pb[:, 0:T].rearrange("p (a b) -> p a b", a=NSEG), [pb], [xr])
            if sample:
                self.cp("pool", convo[:, c, :].rearrange("p (a b) -> p a b", a=4), xr[:, :, LSEG:LSEG + 3], [xr], [convo])
            else:
                self.cp("pool", self.halo[:, c, :], xr[:, 0, LSEG:LSEG + 3], [xr], [self.halo])
            cw = self.convw
            self.ts("dve", ca[:], xr[:, :, 0:LSEG], cw[:, c, 0:1], cw[:, c, 4:5], ALU.mult, ALU.add, [xr, cw], [ca])
            for k in range(1, 4):
                self.stt(ca[:], xr[:, :, k:k + LSEG], cw[:, c, k:k + 1], ca[:], ALU.mult, ALU.add, [xr, cw, ca], [ca])
            self.act(out_ap, ca[:].rearrange("p a b -> p (a b)"), AF.Silu, [ca], [out_buf])

        for wt in range(2 if (pre and tidx < self.n_pre - 1) else 4):
            W = self.wload("w_in", 0, 16, C_B + wt * 512, 512)
            for j in range(4):
                conv_chunk(W, j, 32 + wt * 4 + j, BC[:, wt * 4 + j, :], BC)
        if STOP <= 3:
            return
        tb = 65 * K
        tmp = []
        for i in range(2):
            b0 = tb + i * 16 * K
            d = dict(xs_tm=cv("xs_tm%d" % i, b0, [128, 512], BF16), xw=cv("xw%d" % i, b0 + 2 * K, [128, 512], BF16),
                     Btm=cv("Btm%d" % i, b0 + 4 * K, [128, 128], BF16))
            if not pre:
                d.update(
                    xsD=cv("xsD%d" % i, b0 + 3 * K, [128, 512], BF16), cbt=cv("cbt%d" % i, b0 + 4 * K + 256, [128, 128], BF16),
                    st=cv("sst%d_" % i, b0 + 4 * K + 768, [128, 4], F32),
                    Dm=cv("Dm%d" % i, b0 + 5 * K, [128, 8, 128], BF16), M=cv("M%d" % i, b0 + 9 * K, [128, 8, 128], BF16),
                    t1=cv("t1%d" % i, b0 + 11 * K, [128, 512], F32), junk=cv("junk%d" % i, b0 + 13 * K, [128, 512], BF16),
                    yn=cv("yn%d" % i, b0 + 14 * K, [128, 512], BF16), Sbf=cv("Sbf%d" % i, b0 + 15 * K, [128, 512], BF16))
            tmp.append(d)
        pD = (ps[2], ps[3]); pSm = ps[4]; pY = ps[5]; pZ = ps[6]; pU = ps[7]
        pSm_x = pSm_b = pSm_c = pSm
        pvx = pSm[:].bitcast(BF16)
        step = 0
        wh = {}

        def bulk_unit(g, u):
            xf_, zz_ = xsfm[g % 2], zs[g % 2]
            if u < 4:
                if u == 0:
                    wh["x", g] = self.wload("w_in", 0, 16, C_XS + g * 512, 512)
                conv_chunk(wh["x", g], u, 4 * g + u, xf_[:, u, :], xf_)
            else:
                s_ = u - 4
                if s_ == 0:
                    wh["z", g] = self.wload("w_in", 0, 16, C_Z + g * 512, 512)
                Wz = wh["z", g]
                pb = self.bulk()
                for kc in range(16):
                    self.mm(pb[0:L, :], h_fm[:, kc, s_ * L:(s_ + 1) * L], Wz[:, kc, :], kc == 0, kc == 15, [h_fm, Wz], [pb])
                self.act(zz_[0:L, s_, :], pb[0:L, :], AF.Silu, [pb], [zz_])

        uorder = [0, 1, 2, 3] if pre else [0, 4, 1, 5, 2, 6, 3, 7]
        upstep = len(uorder) // NS
        for u in uorder:
            bulk_unit(0, u)
        for g in range(NG):
            xf = xsfm[g % 2]
            zz = zs[g % 2]
            Sg = self.S[g]
            hs = slice(8 * g, 8 * g + 8)
            for s in range(NS):
                if g + 1 < NG:
                    for u in uorder[s * upstep:(s + 1) * upstep]:
                        bulk_unit(g + 1, u)
                d = tmp[step % 2]
                step += 1
                tok = slice(s * L, (s + 1) * L)
                Sbf = d.get("Sbf")
                if sample:
                    Sg = Ssmp[step % 4]
                    stt_ = sst[step % 4]
                    self.dma("pool", stt_[:], self.sssm[s, 8 * g:8 * g + 8].rearrange("h p n -> (h p) n").rearrange("(j q) n -> q j n", q=128),
                             [], [stt_], ("sst", step % 4))
                    for j in range(4):
                        self.tr(pU[:, j * 128:(j + 1) * 128], stt_[:, j, :], identf[:], [stt_, identf], [pU])
                    self.cp("dve", Sg[:], pU[:], [pU], [Sg])
                if not pre and (s == 0 or sample):
                    self.cp("act", Sbf[:], Sg[:], [Sg], [Sbf])
                for j in range(4):
                    self.tr(pvx[0:L, j * 128:(j + 1) * 128], xf[:, j, tok], ident[:], [xf, ident], [pSm_x])
                xs_tm = d["xs_tm"]
                self.cp("act", xs_tm[0:L, :], pvx[0:L, 0:512], [pSm_x], [xs_tm])
                v3 = lambda b: b[0:L, :].rearrange("p (a b) -> p a b", a=8)
                bc8 = lambda b: b[0:L, s, hs].unsqueeze(2).to_broadcast([L, 8, 64])
                self.tt("dve", v3(d["xw"]), v3(xs_tm), bc8(dtt), ALU.mult, [xs_tm, dtt], [d["xw"]])
                self.tr(pvx[0:L, 512:640], BC[:, g, tok], ident[:], [BC, ident], [pSm_b])
                self.cp("act", d["Btm"][0:L, :], pvx[0:L, 512:640], [pSm_b], [d["Btm"]])
                if not pre:
                    self.tt("pool", v3(d["xsD"]), v3(xs_tm), self.dsk[0:L, hs].unsqueeze(2).to_broadcast([L, 8, 64]), ALU.mult,
                            [xs_tm, self.dsk], [d["xsD"]])
                    self.mm(pSm[0:L, 384:384 + L], BC[:, g, tok], BC[:, 8 + g, tok], True, True, [BC], [pSm_c])
                    self.cp("act", d["cbt"][0:L, 0:L], pSm[0:L, 384:384 + L], [pSm_c], [d["cbt"]])
                    for hh in range(8):
                        h = 8 * g + hh
                        pb = pD[hh // 4]
                        oap = pb[0:L, (hh % 4) * 128:(hh % 4) * 128 + L]
                        self.mm(oap, dahi[0:L, s, h:h + 1].to_broadcast([L, L]), self.trib[0:L, 0:L], True, False, [dahi, self.trib], [pb])
                        self.mm(oap, dalo[0:L, s, h:h + 1].to_broadcast([L, L]), self.trib[0:L, 0:L], False, False, [dalo, self.trib], [pb])
                        self.mm(oap, ident[0:L, 0:L], self.negb[0:L, 0:L], False, True, [ident, self.negb], [pb])
                    Dm, M = d["Dm"], d["M"]
                    for hh in range(8):
                        h = 8 * g + hh
                        pb = pD[hh // 4]
                        self.act(Dm[0:L, hh, 0:L], pb[0:L, (hh % 4) * 128:(hh % 4) * 128 + L], AF.Exp, [pb, dtr], [Dm],
                                 bias=dtr[0:L, s, h:h + 1], scale=1.0)
                    self.tt("dve", M[0:L, :, 0:L], Dm[0:L, :, 0:L], d["cbt"][0:L, 0:L].unsqueeze(1).to_broadcast([L, 8, L]), ALU.mult,
                            [Dm, d["cbt"]], [M])
                    self.mm(pY[0:L, :], ident[0:L, 0:L], d["xsD"][0:L, :], True, False, [ident, d["xsD"]], [pY])
                    for hh in range(8):
                        self.mm(pY[0:L, hh * 64:(hh + 1) * 64], M[0:L, hh, 0:L], xs_tm[0:L, hh * 64:(hh + 1) * 64], False, hh == 7,
                                [M, xs_tm], [pY])
                    self.mm(pZ[0:L, :], BC[:, 8 + g, tok], Sbf[:], True, True, [BC, Sbf], [pZ])
                    t1 = d["t1"]
                    self.tt("dve", v3(t1), pZ[0:L, :].rearrange("p (a b) -> p a b", a=8), bc8(ecum), ALU.mult, [pZ, ecum], [t1])
                    self.tt("dve", t1[0:L, :], t1[0:L, :], pY[0:L, :], ALU.add, [t1, pY], [t1])
                    self.tt("dve", t1[0:L, :], t1[0:L, :], zz[0:L, s, :], ALU.mult, [t1, zz], [t1])
                    st_ = d["st"]
                    self.ms("dve", st_[0:L, 0:1], 0.0, [st_])
                    self.act(d["junk"][0:L, :], t1[0:L, :], AF.Square, [t1], [d["junk"], st_], accum_out=st_[0:L, 0:1])
                    self.rsqrt(st_, st_[0:L, 1:2], st_, st_[0:L, 0:1], 1.0 / 512, L, 1)
                    yn = d["yn"]
                    self.ts("dve", yn[0:L, :], t1[0:L, :], st_[0:L, 1:2], None, ALU.mult, None, [t1, st_], [yn])
                    for j in range(4):
                        self.tr(pvx[:, j * L:(j + 1) * L], yn[0:L, j * 128:(j + 1) * 128], ident[0:L, 0:L], [yn, ident], [pSm_x])
                    self.tt("dve", ynfm[:, 4 * g:4 * g + 4, tok], pvx[:, 0:4 * L].rearrange("p (a b) -> p a b", a=4),
                            self.gssmT[:, 4 * g:4 * g + 4, :].to_broadcast([128, 4, L]), ALU.mult, [pSm_x, self.gssmT], [ynfm])
                self.mm(pU[:, :], d["Btm"][0:L, :], d["xw"][0:L, :], True, True, [d["Btm"], d["xw"]], [pU])
                S3 = Sg[:].rearrange("p (a b) -> p a b", a=8)
                self.tt("dve", S3, S3, edec[:, s, hs].unsqueeze(2).to_broadcast([128, 8, 64]), ALU.mult, [Sg, edec], [Sg])
                self.tt("dve", Sg[:], Sg[:], pU[:, :], ALU.add, [Sg, pU], [Sg])
                if not pre and not sample and s < NS - 1:
                    self.cp("act", tmp[step % 2]["Sbf"][:], Sg[:], [Sg], [tmp[step % 2]["Sbf"]])
                if sample:
                    so = sso[(step - 1) % 4]
                    for j in range(4):
                        self.tr(pU[:, j * 128:(j + 1) * 128], Sg[:, j * 128:(j + 1) * 128], identf[:], [Sg, identf], [pU])
                    self.cp("dve", so[:], pU[:, :].rearrange("p (a b) -> p a b", a=4), [pU], [so])
                    self.dma("pool", self.ssm_s[s, 8 * g:8 * g + 8].rearrange("h p n -> (h p) n").rearrange("(j q) n -> q j n", q=128),
                             so[:], [so], [], ("sso", (step - 1) % 4), is_out=True)
        if sample:
            cst = cv("cst", 65 * K, [16, CD], F32)
            for c0 in range(0, 48, 4):
                pb = self.bulk()
                for c in range(c0, c0 + 4):
                    self.tr(pb[0:12, (c - c0) * 128:(c - c0 + 1) * 128], convo[:, c, :], identf[:], [convo, identf], [pb])
                self.cp("dve", cst[0:12, c0 * 128:(c0 + 4) * 128], pb[0:12, :], [pb], [cst])
            self.dma("pool", self.conv_s.rearrange("q k c -> (q k) c"), cst[0:12, :], [cst], [], "cso", is_out=True)
        if pre or STOP <= 4:
            return
        self.P.tag = "%s%d.p3" % (mode, tidx)
        self.bulk_banks = NBANKS
        vt = [cv("v%d" % s, 16 * K + s * 8 * K, [128, D], F32) for s in range(NS)]
        vn = [cv("vn%d" % s, 48 * K + s * 4 * K, [128, D], BF16) for s in range(NS)]
        ufm = cv("ufm", 64 * K, [128, 16, T], BF16)
        lgB = cv("lgB", 80 * K, [128, D], F32); lbB = cv("lbB", 88 * K, [128, D], F32)
        vst = cv("vst", 96 * K, [128, NS, 8], F32)
        self.dma("pool", lgB[:], self.w["sgu_ln_g"].partition_broadcast(128), [], [lgB], "lgB")
        self.dma("pool", lbB[:], self.w["sgu_ln_b"].partition_broadcast(128), [], [lbB], "lbB")
        self.ms("dve", vst[:], 0.0, [vst])
        for n in range(4):
            Wv = self.wload("w_in", 0, 16, C_V + n * 512, 512)
            for s in range(NS):
                pb = self.bulk()
                for kc in range(16):
                    self.mm(pb[0:L, :], h_fm[:, kc, s * L:(s + 1) * L], Wv[:, kc, :], kc == 0, kc == 15, [h_fm, Wv], [pb])
                self.act(vt[s][0:L, n * 512:(n + 1) * 512], pb[0:L, :], AF.Gelu_apprx_tanh, [pb], [vt[s], vst], accum_out=vst[0:L, s, n:n + 1])
        for s in range(NS):
            v, q = vt[s], vst
            self.act(vn[s][0:L, :], v[0:L, :], AF.Square, [v], [vn[s], q], accum_out=q[0:L, s, 4:5])
            self.red(q[0:L, s, 5:6], q[0:L, s, 0:4], [q], [q])
            self.ts("dve", q[0:L, s, 5:6], q[0:L, s, 5:6], 1.0 / D, None, ALU.mult, None, [q], [q])
            self.tt("dve", q[0:L, s, 6:7], q[0:L, s, 5:6], q[0:L, s, 5:6], ALU.mult, [q], [q])
            self.stt(q[0:L, s, 6:7], q[0:L, s, 4:5], 1.0 / D, q[0:L, s, 6:7], ALU.mult, ALU.subtract, [q], [q])
            self.rsqrt(q, q[0:L, s, 7:8], q, q[0:L, s, 6:7], 1.0, L, 1)
            self.ts("dve", v[0:L, :], v[0:L, :], q[0:L, s, 5:6], q[0:L, s, 7:8], ALU.subtract, ALU.mult, [v, q], [v])
            self.tt("dve", v[0:L, :], v[0:L, :], lgB[0:L, :], ALU.mult, [v, lgB], [v])
            self.tt("dve", v[0:L, :], v[0:L, :], lbB[0:L, :], ALU.add, [v, lbB], [v])
            self.cp("act", vn[s][0:L, :], v[0:L, :], [v], [vn[s]])
            if sample:
                self.dma("pool", self.v_s[s * L:(s + 1) * L, :], v[0:L, :], [v], [], ("vso", s), is_out=True)
        for n in range(4):
            Wu = self.wload("w_in", 0, 16, C_U + n * 512, 512)
            for j in range(4):
                pb = self.bulk()
                for kc in range(16):
                    self.mm(pb[:, 0:T], Wu[:, kc, j * 128:(j + 1) * 128], h_fm[:, kc, :], kc == 0, kc == 15, [Wu, h_fm], [pb])
                self.act(ufm[:, n * 4 + j, :], pb[:, 0:T], AF.Gelu_apprx_tanh, [pb], [ufm])
        for dk in range(16):
            g = dk // 2
            pb = self.bulk()
            for s in range(NS):
                self.mm(pb[:, s * L:(s + 1) * L], vn[s][0:L, dk * 128:(dk + 1) * 128], self.wmaskT[0:L, g, 0:L], True, False, [vn[s], self.wmaskT], [pb])
                self.mm(pb[:, s * L:(s + 1) * L], self.onesb[0:64, :], self.b2[0:64, g * 128:g * 128 + L], False, True, [self.onesb, self.b2], [pb])
            self.tt("dve", ufm[:, dk, :], pb[:, 0:T], ufm[:, dk, :], ALU.mult, [pb, ufm], [ufm])
        if STOP <= 5:
            return
        self.P.tag = "%s%d.p4" % (mode, tidx)
        m1 = cv("m1", 16 * K, [128, 16, T], BF16)
        mg = cv("mg", 32 * K, [128, 16, T], BF16)
        sg = [cv("sg%d" % i, 48 * K + i * 2 * K, [128, T], F32) for i in range(2)]
        sgb = cv("sgb", 52 * K, [128, 4, T], F32)
        for n in range(4):
            Wga = self.wload("w_in", 0, 16, C_GA + n * 512, 512)
            Wa = self.wload("w_a", 0, 16, n * 512, 512)
            for j in range(4):
                jj = n * 4 + j
                pa = self.bulk()
                for kc in range(16):
                    self.mm(pa[:, 0:T], Wga[:, kc, j * 128:(j + 1) * 128], h_fm[:, kc, :], kc == 0, kc == 15, [Wga, h_fm], [pa])
                sgt = sg[j % 2]
                self.act(sgt[:], pa[:, 0:T], AF.Sigmoid, [pa], [sgt])
                pb = self.bulk()
                for kc in range(16):
                    self.mm(pb[:, 0:T], Wa[:, kc, j * 128:(j + 1) * 128], ufm[:, kc, :], kc == 0, kc == 15, [Wa, ufm], [pb])
                self.tt("dve", m1[:, jj, :], sgt[:], pb[:, 0:T], ALU.mult, [sgt, pb], [m1])
        for n in range(4):
            Wgb = self.wload("w_in", 0, 16, C_GB + n * 512, 512)
            for j in range(4):
                pa = self.bulk()
                for kc in range(16):
                    self.mm(pa[:, 0:T], Wgb[:, kc, j * 128:(j + 1) * 128], h_fm[:, kc, :], kc == 0, kc == 15, [Wgb, h_fm], [pa])
                self.act(sgb[:, j, :], pa[:, 0:T], AF.Sigmoid, [pa], [sgb])
            Wb0 = self.wload("w_b", 0, 16, n * 512, 512)
            Wb1 = self.wload("w_b", 2048, 16, n * 512, 512)
            for j in range(4):
                jj = n * 4 + j
                pb = self.bulk()
                for kc in range(32):
                    Wb = Wb0 if kc < 16 else Wb1
                    self.mm(pb[:, 0:T], Wb[:, kc % 16, j * 128:(j + 1) * 128], ynfm[:, kc, :], kc == 0, kc == 31, [Wb, ynfm], [pb])
                sgt = sg[j % 2]
                self.tt("dve", sgt[:], sgb[:, j, :], pb[:, 0:T], ALU.mult, [sgb, pb], [sgt])
                self.tt("dve", mg[:, jj, :], sgt[:], m1[:, jj, :], ALU.add, [sgt, m1], [mg])
        if STOP <= 6:
            return
        self.P.tag = "%s%d.p5" % (mode, tidx)
        if sample:
            L, NS = 64, 1
        xt = [cv("xr%d" % s, 48 * K + s * 8 * K, [128, D], F32) for s in range(NS)]
        xb = [cv("xc%d" % i, 80 * K + i * 4 * K, [128, D], BF16) for i in range(2)]
        stat = [cv("stb%d" % i, 88 * K + i * 64, [128, 4], F32) for i in range(2)]
        for s in range(NS):
            self.dma("pool", xt[s][0:L, :], xsrc[s * L:(s + 1) * L, :], [], [xt[s]], ("xt", s))
        for n in range(4):
            Wo = self.wload("w_o", 0, 16, n * 512, 512)
            for s in range(NS):
                pb = self.bulk()
                for kc in range(16):
                    self.mm(pb[0:L, :], mg[:, kc, s * L:(s + 1) * L], Wo[:, kc, :], kc == 0, kc == 15, [mg, Wo], [pb])
                xs_ = xt[s][0:L, n * 512:(n + 1) * 512]
                self.tt("dve", xs_, xs_, pb[0:L, :], ALU.add, [xt[s], pb], [xt[s]])
        hf = cv("hf", 0, [128, 16, T], BF16)
        for s in range(NS):
            self.norm_to_fm(xt, s, L, self.gffnT, hf, xb[s % 2], stat[s % 2], (ps[2], ps[3]))
        if STOP <= 7:
            return
        self.P.tag = "%s%d.p6" % (mode, tidx)
        afm = cv("afm", 80 * K, [128, 44, T], BF16)
        sg = [cv("sf%d" % i, 16 * K + i * 2 * K, [128, T], F32) for i in range(2)]
        gF = cv("gF", 20 * K, [128, D], F32)
        junk = cv("junkf", 28 * K, [128, D], BF16)
        stf = cv("stf", 32 * K, [128, NS, 2], F32)
        self.dma("pool", gF[:], self.w["norm_final_g"].partition_broadcast(128), [], [gF], "gF")
        for f in range(11):
            Wg = self.wload("w_gate", 0, 16, f * 512, 512)
            Wu = self.wload("w_up", 0, 16, f * 512, 512)
            for j in range(4):
                pa = self.bulk()
                for kc in range(16):
                    self.mm(pa[:, 0:T], Wg[:, kc, j * 128:(j + 1) * 128], hf[:, kc, :], kc == 0, kc == 15, [Wg, hf], [pa])
                sgt = sg[j % 2]
                self.act(sgt[:], pa[:, 0:T], AF.Silu, [pa], [sgt])
                pb = self.bulk()
                for kc in range(16):
                    self.mm(pb[:, 0:T], Wu[:, kc, j * 128:(j + 1) * 128], hf[:, kc, :], kc == 0, kc == 15, [Wu, hf], [pb])
                self.tt("dve", afm[:, f * 4 + j, :], sgt[:], pb[:, 0:T], ALU.mult, [sgt, pb], [afm])
        pdn = (ps[0], ps[1], ps[2], ps[3])
        for n in range(4):
            for kg in range(4):
                Wd_ = self.wload("w_down", kg * 11 * 128, 11, n * 512, 512)
                for s in range(NS):
                    for kc in range(11):
                        self.mm(pdn[s][0:L, :], afm[:, kg * 11 + kc, s * L:(s + 1) * L], Wd_[:, kc, :], kg == 0 and kc == 0,
                                kg == 3 and kc == 10, [afm, Wd_], [pdn[s]])
            for s in range(NS):
                xs_ = xt[s][0:L, n * 512:(n + 1) * 512]
                self.tt("dve", xs_, xs_, pdn[s][0:L, :], ALU.add, [xt[s], pdn[s]], [xt[s]])
        self.ms("dve", stf[:], 0.0, [stf])
        for s in range(NS):
            x = xt[s]
            self.act(junk[0:L, :], x[0:L, :], AF.Square, [x], [junk, stf], accum_out=stf[0:L, s, 0:1])
            self.rsqrt(stf, stf[0:L, s, 1:2], stf, stf[0:L, s, 0:1], 1.0 / D, L, 1)
            self.stt(x[0:L, :], x[0:L, :], stf[0:L, s, 1:2], gF[0:L, :], ALU.mult, ALU.mult, [x, stf, gF], [x])
            self.dma("pool", ydst[s * L:(s + 1) * L, :], x[0:L, :], [x], [], ("yo", s), is_out=True)

    def store_prompt_state(self):
        K = 1024
        so = [self.carve("pso%d" % i, 100 * K + i * 2 * K, [128, 4, 128], F32) for i in range(2)]
        for g in range(NG):
            Sg = self.S[g]
            pU = self.ps[6 + g % 2]
            o = so[g % 2]
            for j in range(4):
                self.tr(pU[:, j * 128:(j + 1) * 128], Sg[:, j * 128:(j + 1) * 128], self.identf[:], [Sg, self.identf], [pU])
            self.cp("dve", o[:], pU[:, :].rearrange("p (a b) -> p a b", a=4), [pU], [o])
            self.dma("pool", self.ssm_p[8 * g:8 * g + 8].rearrange("h p n -> (h p) n").rearrange("(j q) n -> q j n", q=128),
                     o[:], [o], [], ("pso", g % 2), is_out=True)
        cst = self.carve("pcst", 104 * K, [16, CD], F32)
        for c0 in range(0, 48, 4):
            pb = self.bulk()
            for c in range(c0, c0 + 4):
                self.tr(pb[0:3, (c - c0) * 128:(c - c0 + 1) * 128], self.halo[:, c, :], self.identf[:], [self.halo, self.identf], [pb])
            self.cp("dve", cst[0:3, c0 * 128:(c0 + 4) * 128], pb[0:3, :], [pb], [cst])
        self.dma("pool", self.conv_p[:, :], cst[0:3, :], [cst], [], "pco", is_out=True)


_CACHE = {}


def _get_nc(key, **kw):
    if key not in _CACHE:
        _CACHE[key] = Builder(**kw).build()
    return _CACHE[key]


def kernel(x_prompt, x_sample, state_ssm, state_conv, norm_mix_g, w_in, sgu_ln_g, sgu_ln_b, sgu_w, sgu_b, w_a, conv_w, conv_b,
           dt_bias, a_log, d_skip, ssm_norm_g, w_b, w_o, norm_ffn_g, w_gate, w_up, w_down, norm_final_g):
    f = lambda a: np.ascontiguousarray(np.asarray(a, dtype=np.float32))
    nc = _get_nc("full")
    wts = dict(norm_mix_g=f(norm_mix_g[0]), w_in=f(w_in[0]), sgu_ln_g=f(sgu_ln_g[0]), sgu_ln_b=f(sgu_ln_b[0]), sgu_w=f(sgu_w[0]),
               sgu_b=f(sgu_b[0]), w_a=f(w_a[0]), conv_w=f(conv_w[0]), conv_b=f(conv_b[0]), dt_bias=f(dt_bias[0]), a_log=f(a_log[0]),
               d_skip=f(d_skip[0]), ssm_norm_g=f(ssm_norm_g[0]), w_b=f(w_b[0]), w_o=f(w_o[0]), norm_ffn_g=f(norm_ffn_g[0]),
               w_gate=f(w_gate[0]), w_up=f(w_up[0]), w_down=f(w_down[0]), norm_final_g=f(norm_final_g))
    x_prompt = np.asarray(x_prompt); x_sample = np.asarray(x_sample); state_ssm = np.asarray(state_ssm); state_conv = np.asarray(state_conv)
    in_maps = []
    for c in range(8):
        b, half = c // 2, c % 2
        m = dict(wts)
        m["xm"] = f(x_prompt[b, half * 4096:(half + 1) * 4096])
        m["xpre"] = f(x_prompt[b, 0:4096])
        m["flag"] = np.full((128, 1), float(half), np.float32)
        m["xs"] = f(x_sample[4 * c:4 * c + 4].reshape(64, D))
        m["sssm"] = f(state_ssm[0, 4 * c:4 * c + 4])
        m["sconv"] = f(state_conv[0, 4 * c:4 * c + 4])
        in_maps.append(m)
    res = run_bass_kernel_spmd(nc, in_maps, core_ids=list(range(8))).results
    y_prompt = np.empty((4, 8192, D), np.float32)
    for c in range(8):
        y_prompt[c // 2, (c % 2) * 4096:(c % 2 + 1) * 4096] = res[c]["yp"]
    y_sample = np.concatenate([res[c]["ys"].reshape(4, 16, D) for c in range(8)], 0)
    ssm_p = np.stack([res[2 * b + 1]["ssm_p"] for b in range(4)])[None]
    conv_p = np.stack([res[2 * b + 1]["conv_p"] for b in range(4)])[None]
    ssm_s = np.concatenate([res[c]["ssm_s"] for c in range(8)], 0)[None]
    conv_s = np.concatenate([res[c]["conv_s"] for c in range(8)], 0)[None]
    v_s = np.concatenate([res[c]["v_s"].reshape(4, 16, D) for c in range(8)], 0)[None]
    return (y_prompt, y_sample, ssm_p.astype(np.float32), conv_p.astype(np.float32), ssm_s.astype(np.float32),
            conv_s.astype(np.float32), v_s.astype(np.float32))
```
